# Optimizing a Trainium2 kernel written in Bass

```python
import jax, jax.numpy as jnp
from jax import lax
import numpy as np

D_MODEL = 1024
BATCH = 2
SEQ = 8192
DEPTH = 1

CHUNK = 64
PLE_DIM = 256
HEAD_DIM = 64
SWA_HEADS = 8
SWA_KV_HEADS = 2
SWA_GROUP = SWA_HEADS // SWA_KV_HEADS
WINDOW = 128
SWA_BLOCK = WINDOW
FOX_HEADS = 8
FOX_BLOCK = 128
D_FF = 4 * D_MODEL
N_BRANCH = 2
RMS_EPS = 1e-6

SWA_Q = SWA_HEADS * HEAD_DIM
SWA_KV = SWA_KV_HEADS * HEAD_DIM
FOX_W = FOX_HEADS * HEAD_DIM
GATE_W = N_BRANCH * D_MODEL
SPLIT_POINTS = tuple(np.cumsum([SWA_Q, SWA_KV, SWA_KV, FOX_W, FOX_W, FOX_W, FOX_HEADS]).tolist())
D_IN = SPLIT_POINTS[-1] + GATE_W

kernel_name = "hybrid_swa_sink_fox_gated_block"


def alibi_slopes(n_heads):
    return jnp.asarray(np.array([2.0 ** (-8.0 * (h + 1) / n_heads) for h in range(n_heads)], dtype=np.float32))


def rms_norm(x, g):
    x32 = x.astype(jnp.float32)
    y = x32 * lax.rsqrt(jnp.mean(jnp.square(x32), axis=-1, keepdims=True) + RMS_EPS)
    return (y * g.astype(jnp.float32)).astype(x.dtype)


def sliding_window_attention(q, k, v, sinks):
    B, S = q.shape[0], q.shape[1]
    nb = S // SWA_BLOCK
    SB = SWA_BLOCK
    qb = q.reshape(B, nb, SB, SWA_KV_HEADS, SWA_GROUP, HEAD_DIM)
    kb = k.reshape(B, nb, SB, SWA_KV_HEADS, HEAD_DIM)
    vb = v.reshape(B, nb, SB, SWA_KV_HEADS, HEAD_DIM)

    def band(t):
        prev = jnp.pad(t[:, :-1], ((0, 0), (1, 0), (0, 0), (0, 0), (0, 0)))
        return jnp.concatenate([prev, t], axis=2)

    k_band, v_band = band(kb), band(vb)
    s = jnp.einsum('bnqkgd,bnskd->bnkgqs', qb, k_band).astype(jnp.float32) * (HEAD_DIM ** -0.5)

    qi = jnp.arange(SB)[:, None] + SB
    si = jnp.arange(2 * SB)[None, :]
    chunk_diff = qi // CHUNK - si // CHUNK
    band_ok = (chunk_diff >= 0) & (chunk_diff <= WINDOW // CHUNK)
    real_key = (jnp.arange(nb)[:, None, None] > 0) | (si >= SB)[None]
    mask = band_ok[None] & real_key

    slopes = alibi_slopes(SWA_HEADS).reshape(SWA_KV_HEADS, SWA_GROUP)
    alibi = -slopes[:, :, None, None] * jnp.abs(qi - si).astype(jnp.float32)
    s = jnp.where(mask[None, :, None, None], s + alibi[None, None], -jnp.inf)

    sink = jnp.broadcast_to(sinks.astype(jnp.float32).reshape(SWA_KV_HEADS, SWA_GROUP)[None, None, :, :, None, None],
                            s.shape[:-1] + (1,))
    probs = jax.nn.softmax(jnp.concatenate([s, sink], axis=-1), axis=-1)[..., :-1]
    out = jnp.einsum('bnkgqs,bnskd->bnqkgd', probs.astype(v.dtype), v_band)
    return out.reshape(B, S, SWA_Q)


def forgetting_attention(q, k, v, f_logit):
    B, S = q.shape[0], q.shape[1]
    nb = S // FOX_BLOCK
    log_f = jax.nn.log_sigmoid(f_logit.astype(jnp.float32))
    c = jnp.cumsum(log_f, axis=1)
    c_k = c.transpose(0, 2, 1)
    kh = k.transpose(0, 2, 1, 3)
    vh = v.transpose(0, 2, 1, 3)
    q_blocks = q.reshape(B, nb, FOX_BLOCK, FOX_HEADS, HEAD_DIM).transpose(1, 0, 3, 2, 4)
    c_blocks = c.reshape(B, nb, FOX_BLOCK, FOX_HEADS).transpose(1, 0, 3, 2)
    k_pos = jnp.arange(S)
    scale = HEAD_DIM ** -0.5

    def one_block(args):
        qb, cq, n = args
        s = jnp.einsum('bhqd,bhsd->bhqs', qb, kh).astype(jnp.float32) * scale
        s = s + cq[..., None] - c_k[:, :, None, :]
        q_pos = n * FOX_BLOCK + jnp.arange(FOX_BLOCK)
        s = jnp.where((k_pos[None, :] <= q_pos[:, None])[None, None], s, -jnp.inf)
        probs = jax.nn.softmax(s, axis=-1)
        return jnp.einsum('bhqs,bhsd->bhqd', probs.astype(vh.dtype), vh)

    out = lax.map(one_block, (q_blocks, c_blocks, jnp.arange(nb, dtype=jnp.int32)))
    return out.transpose(1, 0, 3, 2, 4).reshape(B, S, FOX_W)


def setup_inputs(seed: int = 0) -> dict:
    key = jax.random.key(seed)
    ks = jax.random.split(key, 18)

    def dense(k, fan_in, fan_out):
        return jax.random.normal(k, (DEPTH, fan_in, fan_out), jnp.float32) * fan_in ** -0.5

    def gain(k, shape):
        return 1.0 + 0.02 * jax.random.normal(k, shape, jnp.float32)

    return {
        "x": jax.random.normal(ks[0], (BATCH, SEQ, D_MODEL), jnp.float32),
        "p": jax.random.normal(ks[1], (DEPTH, BATCH, SEQ, PLE_DIM), jnp.float32),
        "g_mix": gain(ks[2], (DEPTH, D_MODEL)),
        "w_in": dense(ks[3], D_MODEL, D_IN),
        "b_forget": 3.0 + 0.5 * jax.random.normal(ks[4], (DEPTH, FOX_HEADS), jnp.float32),
        "swa_sinks": 0.5 * jax.random.normal(ks[5], (DEPTH, SWA_HEADS), jnp.float32),
        "w_br_swa": dense(ks[6], SWA_Q, D_MODEL),
        "w_br_fox": dense(ks[7], FOX_W, D_MODEL),
        "w_mix_out": dense(ks[8], D_MODEL, D_MODEL),
        "g_mlp": gain(ks[9], (DEPTH, D_MODEL)),
        "w_ff1": dense(ks[10], D_MODEL, D_FF),
        "w_ff2": dense(ks[11], D_FF, D_MODEL),
        "g_ple": gain(ks[12], (DEPTH, D_MODEL)),
        "w_ple_gate": dense(ks[13], D_MODEL, D_MODEL),
        "w_ple_proj": dense(ks[14], PLE_DIM, D_MODEL),
        "g_final": gain(ks[15], (D_MODEL,)),
    }


def reference(x, p, g_mix, w_in, b_forget, swa_sinks, w_br_swa, w_br_fox, w_mix_out,
              g_mlp, w_ff1, w_ff2, g_ple, w_ple_gate, w_ple_proj, g_final):
    B, S = x.shape[0], x.shape[1]
    h = x
    for i in range(DEPTH):
        u = rms_norm(h, g_mix[i])
        z = u @ w_in[i]
        q_a, k_a, v_a, q_b, k_b, v_b, f_b, gate_logits = jnp.split(z, SPLIT_POINTS, axis=-1)
        y_a = sliding_window_attention(
            q_a.reshape(B, S, SWA_HEADS, HEAD_DIM),
            k_a.reshape(B, S, SWA_KV_HEADS, HEAD_DIM),
            v_a.reshape(B, S, SWA_KV_HEADS, HEAD_DIM),
            swa_sinks[i]) @ w_br_swa[i]
        y_b = forgetting_attention(
            q_b.reshape(B, S, FOX_HEADS, HEAD_DIM),
            k_b.reshape(B, S, FOX_HEADS, HEAD_DIM),
            v_b.reshape(B, S, FOX_HEADS, HEAD_DIM),
            f_b + b_forget[i]) @ w_br_fox[i]
        gates = jax.nn.sigmoid(gate_logits).reshape(B, S, N_BRANCH, D_MODEL)
        mixed = gates[:, :, 0] * y_a + gates[:, :, 1] * y_b
        h = h + mixed @ w_mix_out[i]
        u = rms_norm(h, g_mlp[i])
        h = h + jnp.square(jax.nn.relu(u @ w_ff1[i])) @ w_ff2[i]
        ple_gate = jax.nn.sigmoid(rms_norm(h, g_ple[i]) @ w_ple_gate[i])
        h = h + ple_gate * (p[i] @ w_ple_proj[i])
    return rms_norm(h, g_final)
```

```python
from contextlib import ExitStack
from functools import partial
import numpy as np
import ml_dtypes
import concourse.bass as bass
import concourse.mybir as mybir
from concourse.bass_utils import run_bass_kernel_spmd

F32 = mybir.dt.float32
BF16 = mybir.dt.bfloat16
AF = mybir.ActivationFunctionType
ALU = mybir.AluOpType

P = 128
D = 1024
S_LEN = 8192
NB = 64
NSLOT = 16
NOWN = NSLOT * P
D_IN = 4360
DFF = 4096
NEG = -30000.0
ROT = 4000
DEBUG = False


class Sched:
    def __init__(self, nc, ctx):
        self.nc = nc
        self.ctx = ctx
        self.eng = {"pe": nc.tensor, "act": nc.scalar, "dve": nc.vector, "pool": nc.gpsimd, "sp": nc.sync}
        self.csem, self.ccnt, self.dsem, self.dcnt = {}, {}, {}, {}
        self.ops = {k: [] for k in self.eng}
        self.last_w, self.readers = {}, {}
        self.nsem = 0
        self.own = {k: set() for k in self.eng}
        for k in self.eng:
            self._new_csem(k)
        self.dk = {}
        self.bar = []

    def _new_csem(self, k):
        s = self.ctx.enter_context(self.nc.semaphore(f"c{self.nsem}_{k}"))
        self.csem[k] = s
        self.own[k].add(id(s))
        self.ccnt[k] = 0
        self.nsem += 1

    def _new_dsem(self, k):
        self.dsem[k] = self.ctx.enter_context(self.nc.semaphore(f"d{self.nsem}_{k}"))
        self.dcnt[k] = 0
        self.nsem += 1

    def _deps(self, reads, writes, extra):
        deps = {}

        def add(tok):
            if tok is None:
                return
            s, v = tok
            if id(s) not in deps or deps[id(s)][1] < v:
                deps[id(s)] = (s, v)

        for k in reads:
            add(self.last_w.get(k))
        for k in writes:
            add(self.last_w.get(k))
            for t in self.readers.get(k, ()):
                add(t)
        for t in extra:
            add(t)
        for t in self.bar:
            add(t)
        return list(deps.values())

    def _commit(self, tok, reads, writes):
        for k in reads:
            self.readers.setdefault(k, []).append(tok)
        for k in writes:
            self.last_w[k] = tok
            self.readers[k] = []

    def op(self, eng, f, *args, R=(), W=(), X=(), **kw):
        deps = self._deps(R, W, X)
        if eng == "pe":
            deps = [d for d in deps if id(d[0]) not in self.own["pe"]]
        if self.ccnt[eng] >= ROT:
            self._new_csem(eng)
        self.ccnt[eng] += 1
        tok = (self.csem[eng], self.ccnt[eng])
        self.ops[eng].append((deps, partial(f, *args, **kw), tok, 1))
        self._commit(tok, R, W)
        return tok

    def dma(self, q, out, in_, R=(), W=(), X=(), sk=None):
        key = sk if sk is not None else (W[0] if W else ("rd", R[0]))
        if key not in self.dk:
            self.dk[key] = [self.ctx.enter_context(self.nc.semaphore(f"d{self.nsem}")), 0]
            self.nsem += 1
        ent = self.dk[key]
        prev = [(ent[0], ent[1])] if ent[1] > 0 else []
        deps = self._deps(R, W, list(X) + prev)
        ent[1] += 16
        tok = (ent[0], ent[1])
        self.ops[q].append((deps, partial(self.eng[q].dma_start, out=out, in_=in_), tok, 16))
        self._commit(tok, R, W)
        return tok

    def barrier(self):
        bar = []
        for k in self.eng:
            if self.ccnt[k] > 0:
                bar.append((self.csem[k], self.ccnt[k]))
        for ent in self.dk.values():
            if ent[1] > 0:
                bar.append((ent[0], ent[1]))
        self.bar = bar
        self.last_w, self.readers = {}, {}

    def emit(self, final_tokens):
        nc = self.nc
        ops = self.ops
        fin = {}
        for s, v in final_tokens:
            if id(s) not in fin or fin[id(s)][1] < v:
                fin[id(s)] = (s, v)

        def run(k, e):
            seen = {}
            for deps, fn, tok, inc in ops[k]:
                for s, v in deps:
                    if seen.get(id(s), 0) >= v:
                        continue
                    e.wait_ge(s, v)
                    seen[id(s)] = v
                fn().then_inc(tok[0], inc)
            if k == "sp":
                for s, v in fin.values():
                    e.wait_ge(s, v)

        with nc.Block() as block:
            @block.sync
            def _(e):
                run("sp", e)

            @block.tensor
            def _(e):
                run("pe", e)

            @block.scalar
            def _(e):
                run("act", e)

            @block.vector
            def _(e):
                run("dve", e)

            @block.gpsimd
            def _(e):
                run("pool", e)


class NS:
    def __init__(self, **kw):
        self.__dict__.update(kw)


ARENA_KB = 207.75
NCONST = 128 * 3 + 512 + 4096
C_ID, C_TRI, C_ONE, C_MF, C_SW = 0, 128, 256, 384, 896


def build_program():
    nc = bass.Bass("TRN2", target_bir_lowering=False)
    V, A, T, G = nc.vector, nc.scalar, nc.tensor, nc.gpsimd

    def din(name, shape, dt=F32):
        return nc.dram_tensor(name, list(shape), dt, kind="ExternalInput").ap()

    x_all = din("x_all", [S_LEN, D])
    x_own = din("x_own", [NOWN, D])
    x_halo = din("x_halo", [NOWN, D])
    p_own = din("p_own", [NOWN, 256])
    w_in = din("w_in", [D, D_IN])
    w_br_swa = din("w_br_swa", [512, D])
    w_br_fox = din("w_br_fox", [512, D])
    w_mix = din("w_mix_out", [D, D])
    w_ff1 = din("w_ff1", [D, DFF])
    w_ff2 = din("w_ff2", [DFF, D])
    w_pg = din("w_ple_gate", [D, D])
    w_pp = din("w_ple_proj", [256, D])
    smalls = din("smalls", [P, 40])
    gfin_d = din("gfin", [P, D])
    cbf_d = din("cbf", [P, NCONST], BF16)
    c64_d = din("c64", [64, 64 + NOWN], BF16)
    out_d = nc.dram_tensor("out", [NOWN, D], F32, kind="ExternalOutput").ap()
    uT_all_d = nc.dram_tensor("uT_all_scr", [16, P, 8 * 512], BF16, kind="Internal").ap()
    uT_own_d = nc.dram_tensor("uT_own_scr", [4, P, 8 * 512], BF16, kind="Internal").ap()
    if DEBUG:
        dbg_oa = nc.dram_tensor("dbg_oa", [P, 4 * NOWN], BF16, kind="ExternalOutput").ap()
        dbg_ob = nc.dram_tensor("dbg_ob", [P, 4 * NOWN], BF16, kind="ExternalOutput").ap()
    if DEBUG == 2:
        dbg_mx = nc.dram_tensor("dbg_mx", [P, 8 * NOWN], BF16, kind="ExternalOutput").ap()
        dbg_h1 = nc.dram_tensor("dbg_h1", [P, NSLOT * D], F32, kind="ExternalOutput").ap()
        dbg_h2 = nc.dram_tensor("dbg_h2", [P, NSLOT * D], F32, kind="ExternalOutput").ap()
        dbg_h3 = nc.dram_tensor("dbg_h3", [P, NSLOT * D], F32, kind="ExternalOutput").ap()

    with ExitStack() as ctx:
        arena = ctx.enter_context(nc.sbuf_tensor("arena", [P, int(ARENA_KB * 512)], BF16))
        psb = [ctx.enter_context(nc.psum_tensor(f"ps{i}", [P, 512], F32)) for i in range(8)]
        S = Sched(nc, ctx)

        class Region:
            def __init__(self, start_kb, end_kb):
                self.off = int(start_kb * 1024)
                self.end = int(end_kb * 1024)

            def take(self, nbytes, parts=P, dt=BF16):
                nbytes = (nbytes + 31) // 32 * 32
                o = self.off
                self.off += nbytes
                assert self.off <= self.end, (self.off, self.end)
                v = arena[0:parts, o // 2:(o + nbytes) // 2]
                if dt == F32:
                    v = v.bitcast(F32)
                return v

        def bf(region, ncols, parts=P):
            return region.take(ncols * 2, parts, BF16)[:, 0:ncols]

        def f32(region, ncols, parts=P):
            return region.take(ncols * 4, parts, F32)[:, 0:ncols]

        RC = Region(0, 26)
        cbf = bf(RC, NCONST)
        c64 = bf(RC, 64 + NOWN, 64)
        sm = f32(RC, 40)
        gfin = f32(RC, D)
        negc = f32(RC, 512)
        onesf = f32(RC, 64)
        sinkexp = f32(RC, 8)
        sinkrow = f32(RC, 1024)
        onesf2 = f32(RC, 128)
        stat = f32(RC, 3 * 8)
        eps_t = f32(RC, 1)
        dummy_t = f32(RC, 1)
        lown_keep = bf(RC, 3 * 128).rearrange("p (i n) -> p i n", i=3)
        identb = cbf[:, C_ID:C_ID + 128]
        triU = cbf[:, C_TRI:C_TRI + 128]
        onesb = cbf[:, C_ONE:C_ONE + 128]
        maskF = cbf[:, C_MF:C_MF + 512].rearrange("p (s q) -> p s q", s=4)
        swab = cbf[:, C_SW:C_SW + 4096].rearrange("p (f h t q) -> p f h t q", f=2, h=8, t=2)
        Wall = c64[:, 0:64]
        Wb = c64[:, 64:64 + NOWN]

        S.dma("sp", cbf, cbf_d, W=["cbf"])
        S.dma("sp", c64, c64_d, W=["c64"])
        S.dma("sp", sm, smalls, W=["sm"])
        S.dma("sp", gfin, gfin_d, W=["gfin"])
        S.op("dve", V.memset, onesf, 1.0, W=["onesf"])
        S.op("dve", V.memset, onesf2, 1.0, W=["onesf2"])
        S.op("dve", V.memset, eps_t, 1e-6, W=["eps"])
        S.op("act", A.activation, sinkexp, sm[:, 32:40], AF.Exp, R=["sm"], W=["sinkexp"])
        for hh_ in range(8):
            S.op("dve", V.tensor_scalar, sinkrow[64:65, hh_ * 128:(hh_ + 1) * 128], onesf2[64:65, :], sinkexp[64:65, hh_:hh_ + 1], None,
                 ALU.mult, R=["sinkexp", "onesf2"], W=["sinkrow"])

        oaT = arena[:, 26 * 512:26 * 512 + 4 * NOWN].rearrange("p (c t) -> p c t", c=4)
        obT = arena[:, 42 * 512:42 * 512 + 4 * NOWN].rearrange("p (c t) -> p c t", c=4)
        Qaug = arena[:, 58 * 512:58 * 512 + 8 * NOWN].rearrange("p (h t) -> p h t", h=8)

        ctr = NS(stat=0, fm=0, tile=0, unit=0)

        def norm_a(src, skey, B, i):
            ctr.stat += 1
            sl = ctr.stat % 8
            ss = stat[:, sl * 3:sl * 3 + 1]
            ln = stat[:, sl * 3 + 1:sl * 3 + 2]
            rs = stat[:, sl * 3 + 2:sl * 3 + 3]
            k = ("stat", sl)
            xs, xkey = B.xs[i % len(B.xs)], ("xs", i % len(B.xs))
            junk, jkey = B.junk[0], ("junk", 0)
            S.op("act", A.activation, junk, src, AF.Square, accum_out=ss, R=[skey], W=[jkey, k])
            S.op("act", A.activation, dummy_t, eps_t, AF.Copy, R=["eps"], W=[k, "dummy"])
            S.op("act", A.activation, ln, ss, AF.Ln, bias=eps_t, scale=1.0 / D, R=[k, "eps"], W=[k])
            S.op("act", A.activation, rs, ln, AF.Exp, scale=-0.5, R=[k], W=[k])
            S.op("dve", V.tensor_scalar, xs, src, rs, None, ALU.mult, R=[skey, k], W=[xkey])
            return xs, xkey

        def norm_b(xs, xkey, dst3, dkeys, pbank):
            pk = ("ps", pbank)
            pv = psb[pbank][:].bitcast(BF16).rearrange("p (c t) -> p c t", c=8)
            for c in range(8):
                S.op("pe", T.transpose, pv[:, c, :], xs[:, c * 128:(c + 1) * 128], identb, R=[xkey, "cbf"], W=[pk])
            S.op("dve", V.tensor_copy, dst3, pv, R=[pk], W=dkeys)

        def norm_tile(src, skey, B, i, dst3, dkeys, pbank):
            xs, xkey = norm_a(src, skey, B, i)
            norm_b(xs, xkey, dst3, dkeys, pbank)

        def stream_chunk(src_d, chunk, B, ub, cache_d=None, cname=None):
            pendb = []
            for t in range(4):
                i = ctr.tile
                ctr.tile += 1
                xi, xk = B.xin[i % len(B.xin)], ("xin", i % len(B.xin))
                r0 = chunk * 512 + t * 128
                S.dma("sp", xi, src_d[r0:r0 + 128, :], W=[xk])
                xs, xkey = norm_a(xi, xk, B, i)
                if pendb:
                    norm_b(*pendb.pop())
                pendb.append((xs, xkey, B.uT[ub][:, :, t * 128:(t + 1) * 128], [("uT", ub, t)], 6 + (i % 2)))
            norm_b(*pendb.pop())
            if cache_d is not None:
                S.dma("pool", cache_d[chunk], B.uT[ub].rearrange("p c t -> p (c t)"), R=[("uT", ub, t) for t in range(4)],
                      W=[("uTd", cname, chunk)], sk=("cw", ub))

        def proj_fm(B, ub, w, wkey, col0, evacs, bank=None):
            if bank is None:
                ctr.fm += 1
                bank = ctr.fm % 2
            pk = ("ps", bank)
            for c in range(8):
                S.op("pe", T.matmul, psb[bank][:], w[:, c, col0:col0 + 128], B.uT[ub][:, c, :], start=(c == 0), stop=(c == 7),
                     R=[(wkey, c)] + [("uT", ub, t) for t in range(4)], W=[pk])
            for (eng, dst, dkey, prow, scale) in evacs:
                src = psb[bank][prow:prow + 64, :]
                if eng == "act":
                    S.op("act", A.mul, dst, src, float(scale), R=[pk], W=[dkey])
                else:
                    S.op("dve", V.tensor_scalar, dst, src, float(scale), None, ALU.mult, R=[pk], W=[dkey])

        def normalize_p1(B, obank, sink_g, dests, idx):
            ok = ("ps", obank)
            rr, rk = B.rrow[idx % 2], ("rrow", idx % 2)
            nt, nk = B.numt[idx % 2], ("numt", idx % 2)
            if sink_g is not None:
                S.op("dve", V.tensor_tensor, rr[64:65, :], psb[obank][64:65, :], sinkrow[64:65, sink_g * 512:(sink_g + 1) * 512],
                     ALU.add, R=[ok, "sinkrow"], W=[rk])
                S.op("act", A.activation, rr[64:65, :], rr[64:65, :], AF.Ln, R=[rk], W=[rk])
                S.op("act", A.activation, rr[64:65, :], rr[64:65, :], AF.Exp, scale=-1.0, R=[rk], W=[rk])
            else:
                S.op("dve", V.reciprocal, rr[64:65, :], psb[obank][64:65, :], R=[ok], W=[rk])
            S.op("dve", V.tensor_copy, nt[0:64, :], psb[obank][0:64, :], R=[ok], W=[nk])

        def normalize_p2(B, obank, sink_g, dests, idx):
            rr, rk = B.rrow[idx % 2], ("rrow", idx % 2)
            nt, nk = B.numt[idx % 2], ("numt", idx % 2)
            bk = ("ps", 5)
            S.op("pe", T.matmul, psb[5][0:64, :], onesf[64:65, 0:64], rr[64:65, :], start=True, stop=True,
                 R=[rk, "onesf"], W=[bk])
            for (dst, vf, dkeys) in dests:
                S.op("dve", V.tensor_tensor, dst, vf(nt[0:64, :]), vf(psb[5][0:64, :]), ALU.mult,
                     R=[nk, bk], W=dkeys)

        def normalize_rows(B, obank, sink_g, dests, idx):
            normalize_p1(B, obank, sink_g, dests, idx)
            normalize_p2(B, obank, sink_g, dests, idx)

        def split3(z, zkey, dst3, dkey, n, parts, B):
            t0, t1 = B.tmpf[0], B.tmpf[1]
            S.op("dve", V.tensor_copy, dst3[0:parts, 0, :], z, R=[zkey], W=[dkey])
            S.op("dve", V.tensor_tensor, t0[0:parts, 0:n], z, dst3[0:parts, 0, :], ALU.subtract, R=[zkey, dkey], W=["t0"])
            S.op("dve", V.tensor_copy, dst3[0:parts, 1, :], t0[0:parts, 0:n], R=["t0"], W=[dkey])
            S.op("dve", V.tensor_tensor, t1[0:parts, 0:n], t0[0:parts, 0:n], dst3[0:parts, 1, :], ALU.subtract,
                 R=["t0", dkey], W=["t1"])
            S.op("dve", V.tensor_copy, dst3[0:parts, 2, :], t1[0:parts, 0:n], R=["t1"], W=[dkey])

        def softplus_split(z, zkey, dst3, dkey, n, parts, B):
            S.op("act", A.activation, z, z, AF.Exp, scale=-1.0, R=[zkey], W=[zkey])
            S.op("act", A.activation, z, z, AF.Ln, bias=1.0, R=[zkey], W=[zkey])
            split3(z, zkey, dst3, dkey, n, parts, B)

        def vproj(B, ub, t, w, wkey, c0, ncol, bank):
            pk = ("ps", bank)
            for c in range(8):
                S.op("pe", T.matmul, psb[bank][:, 0:ncol], B.uT[ub][:, c, t * 128:(t + 1) * 128], w[:, c, c0:c0 + ncol],
                     start=(c == 0), stop=(c == 7), R=[(wkey, c), ("uT", ub, t)], W=[pk])
            return pk

        R1 = Region(90, ARENA_KB)
        wq = bf(R1, 8 * 1288).rearrange("p (c n) -> p c n", c=8)
        wst = [f32(R1, 1280) for _ in range(2)]
        B1 = NS(xin=[f32(R1, D) for _ in range(4)], xs=[bf(R1, D) for _ in range(3)], junk=[bf(R1, D) for _ in range(1)],
                uT=[bf(R1, 8 * 512).rearrange("p (c t) -> p c t", c=8) for _ in range(2)],
                rrow=[f32(R1, 512) for _ in range(2)], numt=[f32(R1, 512) for _ in range(2)],
                tmpf=[f32(R1, 512) for _ in range(2)])
        qaT = bf(R1, 8 * 512, 64).rearrange("p (h t) -> p h t", h=8)
        kaT = bf(R1, 4 * 512, 64).rearrange("p (g s t) -> p g s t", g=2, s=2)
        vaA = bf(R1, 2 * 4 * 2 * 65).rearrange("p (s m g d) -> p s m g d", s=2, m=4, g=2)
        PT1 = [bf(R1, 512) for _ in range(3)]
        zb_own = f32(R1, 128)
        l_own3 = bf(R1, 3 * 128).rearrange("p (i n) -> p i n", i=3)

        for c in range(8):
            st, sk = wst[c % 2], ("wst", c % 2)
            S.dma("sp", st[:, 0:1280], w_in[c * 128:(c + 1) * 128, 0:1280], W=[sk])
            S.op("dve", V.tensor_scalar, wq[:, c, 0:1280], st[:, 0:1280], sm[:, c:c + 1], None, ALU.mult,
                 R=[sk, "sm"], W=[("wq", c)])
            S.dma("sp", st[:, 0:8], w_in[c * 128:(c + 1) * 128, 2304:2312], W=[sk])
            S.op("dve", V.tensor_scalar, wq[:, c, 1280:1288], st[:, 0:8], sm[:, c:c + 1], None, ALU.mult,
                 R=[sk, "sm"], W=[("wq", c)])
        S.op("pool", G.memset, vaA[:, :, :, :, 64:65], 1.0, W=["vaA1"])

        pend = []
        for a in range(4):
            ub = 0
            stream_chunk(x_own, a, B1, ub, uT_own_d, "own")
            for hp in range(4):
                proj_fm(B1, ub, wq, "wq", 768 + hp * 128, [
                    ("act", Qaug[0:64, 2 * hp, a * 512:(a + 1) * 512], ("Qaug", 2 * hp, a), 0, 0.125),
                    ("dve", Qaug[0:64, 2 * hp + 1, a * 512:(a + 1) * 512], ("Qaug", 2 * hp + 1, a), 64, 0.125)])
            for hp in range(4):
                proj_fm(B1, ub, wq, "wq", hp * 128, [
                    ("act", qaT[0:64, 2 * hp, :], ("qaT", 2 * hp), 0, 0.125),
                    ("dve", qaT[0:64, 2 * hp + 1, :], ("qaT", 2 * hp + 1), 64, 0.125)])
            proj_fm(B1, ub, wq, "wq", 512, [
                ("act", kaT[0:64, 0, 1, :], ("kaT", 0, 1), 0, 1.0),
                ("dve", kaT[0:64, 1, 1, :], ("kaT", 1, 1), 64, 1.0)])
            for t in range(4):
                bank = 2 + (t % 2)
                pk = vproj(B1, ub, t, wq, "wq", 640, 128, bank)
                for c in range(8):
                    S.op("pe", T.matmul, psb[bank][:, 128:136], B1.uT[ub][:, c, t * 128:(t + 1) * 128], wq[:, c, 1280:1288],
                         start=(c == 0), stop=(c == 7), R=[("wq", c), ("uT", ub, t)], W=[pk])
                S.op("dve", V.tensor_copy, vaA[:, 1, t, :, 0:64], psb[bank][:, 0:128].rearrange("p (g d) -> p g d", g=2),
                     R=[pk], W=[("vaA", 1, t)])
                m = a * 4 + t
                S.op("dve", V.tensor_tensor, zb_own[:, m * 8:(m + 1) * 8], psb[bank][:, 128:136], sm[:, 24:32], ALU.add,
                     R=[pk, "sm"], W=["zb_own"])
            ub = 1
            stream_chunk(x_halo, a, B1, ub, None)
            proj_fm(B1, ub, wq, "wq", 512, [
                ("act", kaT[0:64, 0, 0, :], ("kaT", 0, 0), 0, 1.0),
                ("dve", kaT[0:64, 1, 0, :], ("kaT", 1, 0), 64, 1.0)])
            for t in range(4):
                bank = 2 + (t % 2)
                pk = vproj(B1, ub, t, wq, "wq", 640, 128, bank)
                S.op("dve", V.tensor_copy, vaA[:, 0, t, :, 0:64], psb[bank][:, 0:128].rearrange("p (g d) -> p g d", g=2),
                     R=[pk], W=[("vaA", 0, t)])
            for s in range(4):
                m = a * 4 + s
                first = 0 if m == 0 else 1
                for g in range(2):
                    u = ctr.unit
                    ctr.unit += 1
                    obank = 4 if (u % 2 == 0) else 1
                    ok = ("ps", obank)
                    for t in range(2):
                        sbank = 2 + t
                        sk = ("ps", sbank)
                        sv = psb[sbank][:].rearrange("p (h q) -> p h q", h=4)
                        S.op("pe", T.matmul, sv, kaT[0:64, g, t, s * 128:(s + 1) * 128],
                             qaT[0:64, 4 * g:4 * g + 4, s * 128:(s + 1) * 128], start=True, stop=False,
                             R=[("kaT", g, t)] + [("qaT", 4 * g + i) for i in range(4)], W=[sk])
                        S.op("pe", T.matmul, sv, identb, swab[:, first, 4 * g:4 * g + 4, t, :], start=False, stop=True,
                             R=["cbf"], W=[sk])
                    for t in range(2):
                        sbank = 2 + t
                        sk = ("ps", sbank)
                        pt, ptk = PT1[(2 * u + t) % 3], ("PT", (2 * u + t) % 3)
                        S.op("act", A.activation, pt, psb[sbank][:], AF.Exp, R=[sk], W=[ptk])
                        S.op("pe", T.matmul, psb[obank][0:65, :], vaA[:, t, s, g, :], pt, start=(t == 0), stop=(t == 1),
                             R=[ptk, ("vaA", t, s), "vaA1"], W=[ok])
                    dests = []
                    for par in range(2):
                        dests.append((oaT[par * 64:par * 64 + 64, 2 * g:2 * g + 2, m * 128:(m + 1) * 128],
                                      (lambda v, par=par: v.rearrange("p (c e q) -> p c e q", c=2, e=2)[:, :, par, :]),
                                      [("oaT", 4 * g + par, m), ("oaT", 4 * g + 2 + par, m)]))
                    if pend:
                        normalize_rows(*pend.pop())
                    pend.append((B1, obank, g, dests, u))
            if pend:
                normalize_rows(*pend.pop())

        softplus_split(zb_own, "zb_own", l_own3, "l_own3", 128, P, B1)
        S.op("dve", V.tensor_copy, lown_keep, l_own3, R=["l_own3"], W=["lown_keep"])
        if DEBUG:
            S.dma("pool", dbg_oa, oaT.rearrange("p c t -> p (c t)"), R=[("oaT", h, m) for h in range(8) for m in range(16)])

        S.barrier()
        R2 = Region(90, ARENA_KB)
        wkv = bf(R2, 8 * 264).rearrange("p (c n) -> p c n", c=8)
        wst2 = [f32(R2, 264) for _ in range(2)]
        kaug = bf(R2, 2 * S_LEN).rearrange("p (h t) -> p h t", h=2)
        vaug = bf(R2, NB * 2 * 65).rearrange("p (b h d) -> p b h d", b=NB, h=2)
        B2 = NS(xin=[f32(R2, D) for _ in range(4)], xs=[bf(R2, D) for _ in range(2)], junk=[bf(R2, D) for _ in range(1)],
                uT=[bf(R2, 8 * 512).rearrange("p (c t) -> p c t", c=8) for _ in range(2)],
                rrow=[f32(R2, 512) for _ in range(2)], numt=[f32(R2, 512) for _ in range(2)])
        B2.tmpf = B2.numt
        PT2 = [bf(R2, 512) for _ in range(4)]
        zb = f32(R2, 512)
        l3 = bf(R2, 3 * 512).rearrange("p (i n) -> p i n", i=3)
        totT = f32(R2, 8, 64)
        tot3 = bf(R2, 3 * 8, 64).rearrange("p (i h) -> p i h", i=3)
        RT3 = bf(R2, 3 * 512, 64).rearrange("p (i n) -> p i n", i=3)
        crow = bf(R2, NOWN, 8)

        S.op("dve", V.memset, kaug[64:65, :, :], 1.0, W=["kaug1"])
        S.op("pool", G.memset, vaug[:, :, :, 64:65], 1.0, W=["vaug1"])

        ctr.sc = 0

        def fox_weights(hp):
            kc0 = 1280 + hp * 128
            vc0 = 1792 + hp * 128
            for c in range(8):
                st, sk = wst2[c % 2], ("wst2", c % 2)
                for (d0, s0, n) in ((0, kc0, 128), (128, vc0, 128), (256, 2304, 8)):
                    S.dma("sp", st[:, d0:d0 + n], w_in[c * 128:(c + 1) * 128, s0:s0 + n], W=[sk])
                S.op("dve", V.tensor_scalar, wkv[:, c, :], st[:, 0:264], sm[:, c:c + 1], None, ALU.mult,
                     R=[sk, "sm"], W=[("wkv", c)])

        def fox_stream_chunk(hp, ch, overl):
            ub = ctr.sc % 2
            ctr.sc += 1
            if hp == 0:
                stream_chunk(x_all, ch, B2, ub, uT_all_d, "all")
            else:
                S.dma("sp", B2.uT[ub].rearrange("p c t -> p (c t)"), uT_all_d[ch], R=[("uTd", "all", ch)],
                      W=[("uT", ub, t) for t in range(4)])
            proj_fm(B2, ub, wkv, "wkv", 0, [
                ("dve" if overl else "act", kaug[0:64, 0, ch * 512:(ch + 1) * 512], ("kaug", 0, ch), 0, 1.0),
                ("dve", kaug[0:64, 1, ch * 512:(ch + 1) * 512], ("kaug", 1, ch), 64, 1.0)], bank=(6 if overl else None))
            for t in range(4):
                bank = (7 - (t % 2)) if overl else (2 + (t % 2))
                blk = ch * 4 + t
                ncol = 136 if hp == 0 else 128
                pk = vproj(B2, ub, t, wkv, "wkv", 128, ncol, bank)
                S.op("dve", V.tensor_copy, vaug[:, blk, :, 0:64], psb[bank][:, 0:128].rearrange("p (g d) -> p g d", g=2),
                     R=[pk], W=[("vaug", blk)])
                if hp == 0:
                    S.op("dve", V.tensor_tensor, zb[:, blk * 8:(blk + 1) * 8], psb[bank][:, 128:136], sm[:, 24:32], ALU.add,
                         R=[pk, "sm"], W=["zb"])

        class NextStream:
            def __init__(self, hp):
                self.hp, self.q, self.started = hp, [], False

            def release(self, quarter):
                self.q.extend(range(4 * quarter, 4 * quarter + 4))

            def maybe(self):
                if self.q:
                    if not self.started:
                        fox_weights(self.hp)
                        self.started = True
                    fox_stream_chunk(self.hp, self.q.pop(0), True)

            def flush(self):
                while self.q:
                    self.maybe()

        def fox_decay():
            softplus_split(zb, "zb", l3, "l3", 512, P, B2)
            tk = ("ps", 6)
            l3v = l3.rearrange("p i (b h) -> p i b h", h=8)
            for h in range(8):
                for i in range(3):
                    S.op("pe", T.matmul, psb[6][0:64, h:h + 1], l3v[:, i, :, h], onesb[:, 0:1], start=(i == 0), stop=(i == 2),
                         R=["l3", "cbf"], W=[tk])
            S.op("dve", V.tensor_copy, totT, psb[6][0:64, 0:8], R=[tk], W=["totT"])
            split3(totT, "totT", tot3, "tot3", 8, 64, B2)
            RT3v = RT3.rearrange("p i (b h) -> p i b h", h=8)
            for i in range(3):
                for h in range(8):
                    S.op("dve", V.tensor_scalar, RT3v[:, i, :, h], Wall, tot3[:, i, h:h + 1], None, ALU.mult,
                         R=["c64", "tot3"], W=["RT3"])
            ck = ("ps", 7)
            for i in range(3):
                S.op("pe", T.matmul, psb[7][:], triU, l3[:, i, :], start=(i == 0), stop=False, R=["l3", "cbf"], W=[ck])
            for i in range(3):
                S.op("pe", T.matmul, psb[7][:], onesb[0:64, :], RT3[:, i, :], start=False, stop=(i == 2),
                     R=["RT3", "cbf"], W=[ck])
            S.op("dve", V.tensor_copy, negc, psb[7][:], R=[ck], W=["negc"])
            lov = lown_keep.rearrange("p i (m h) -> p i m h", h=8)
            for q4 in range(4):
                bk = ("ps", 6)
                for mm in range(4):
                    m = q4 * 4 + mm
                    for i in range(3):
                        S.op("pe", T.matmul, psb[6][0:8, mm * 128:(mm + 1) * 128], lov[:, i, m, :], triU,
                             start=(i == 0), stop=False, R=["lown_keep", "cbf"], W=[bk])
                    for i in range(3):
                        S.op("pe", T.matmul, psb[6][0:8, mm * 128:(mm + 1) * 128], tot3[:, i, :], Wb[:, m * 128:(m + 1) * 128],
                             start=False, stop=(i == 2), R=["tot3", "c64"], W=[bk])
                S.op("dve", V.tensor_scalar, crow[:, q4 * 512:(q4 + 1) * 512], psb[6][0:8, :], -1.0, None, ALU.mult,
                     R=[bk], W=["crow"])
            for h in range(8):
                S.dma("pool", Qaug[64:65, h, :], crow[h:h + 1, :], R=["crow"], W=[("Qrow", h)])


        def fox_attention(hp, nxt):
            it = 0
            for a in (3, 2, 1, 0):
                for hh in range(2):
                    h = 2 * hp + hh
                    u = ctr.unit
                    ctr.unit += 1
                    obank = 3 + (u % 2)
                    ok = ("ps", obank)
                    kbs = [(kb, 0, None) for kb in range(16 * a)]
                    for r in range(4):
                        for s_ in range(4):
                            kbs.append((16 * a + 4 * r + s_, r * 128, s_))
                    qreads = [("Qaug", h, a), ("Qrow", h)]

                    def emit_s(n_i):
                        kb, c0, ms = kbs[n_i]
                        ncl = 512 - c0
                        sbank = n_i % 3
                        sk = ("ps", sbank)
                        S.op("pe", T.matmul, psb[sbank][:, 0:ncl], kaug[0:65, hh, kb * 128:(kb + 1) * 128],
                             Qaug[0:65, h, a * 512 + c0:(a + 1) * 512], start=True, stop=(ms is None),
                             R=[("kaug", hh, kb // 4), "kaug1"] + qreads, W=[sk])
                        if ms is not None:
                            S.op("pe", T.matmul, psb[sbank][:, 0:128], identb, maskF[:, ms, :], start=False, stop=True,
                                 R=["cbf"], W=[sk])

                    def emit_rest(n_i):
                        kb, c0, ms = kbs[n_i]
                        ncl = 512 - c0
                        sbank = n_i % 3
                        sk = ("ps", sbank)
                        pt, ptk = PT2[n_i % 4], ("PT", n_i % 4)
                        S.op("act", A.activation, pt[:, 0:ncl], psb[sbank][:, 0:ncl], AF.Exp,
                             bias=negc[:, kb * 8 + h:kb * 8 + h + 1], R=[sk, "negc"], W=[ptk])
                        S.op("pe", T.matmul, psb[obank][0:65, c0:512], vaug[:, kb, hh, :], pt[:, 0:ncl],
                             start=(n_i == 0), stop=(n_i == len(kbs) - 1), R=[ptk, ("vaug", kb), "vaug1"], W=[ok])

                    for n_i in range(len(kbs) + 2):
                        if n_i < len(kbs):
                            emit_s(n_i)
                        if n_i >= 2:
                            emit_rest(n_i - 2)
                        it += 1
                        if nxt is not None and it % 8 == 0:
                            nxt.maybe()
                        if pend and n_i == 3:
                            normalize_p1(*pend[0])
                        if pend and n_i == 12:
                            normalize_p2(*pend.pop())
                    pend.append((B2, obank, None,
                                 [(obT[hh * 64:hh * 64 + 64, hp, a * 512:(a + 1) * 512], (lambda v: v), [("obT", h, a)])], u))
                if nxt is not None:
                    nxt.release(a)
            if pend:
                normalize_rows(*pend.pop())
            if nxt is not None:
                nxt.flush()

        fox_weights(0)
        for ch in range(16):
            fox_stream_chunk(0, ch, False)
        fox_decay()
        for hp in range(4):
            fox_attention(hp, NextStream(hp + 1) if hp < 3 else None)

        if DEBUG == 2:
            S.dma("pool", dbg_ob, obT.rearrange("p c t -> p (c t)"), R=[("obT", h, a) for h in range(8) for a in range(4)])
        if DEBUG == 1:
            tk1 = S.dma("pool", dbg_ob, obT.rearrange("p c t -> p (c t)"), R=[("obT", h, a) for h in range(8) for a in range(4)])
            tk2 = S.dma("pool", out_d[0:128, :], gfin, R=["gfin"])
            S.emit([tk1, tk2])
            return nc

        S.barrier()
        mixedT = arena[:, 164 * 512:164 * 512 + 8 * NOWN].rearrange("p (c t) -> p c t", c=8)
        R3 = Region(58, 164)
        wbr = bf(R3, 2 * 4 * D).rearrange("p (b c n) -> p b c n", b=2, c=4)
        wg = bf(R3, 8 * 2048).rearrange("p (c n) -> p c n", c=8)
        st3 = [f32(R3, 2048) for _ in range(2)]
        uT3 = [bf(R3, 8 * 512).rearrange("p (c t) -> p c t", c=8) for _ in range(2)]
        th3 = [f32(R3, 512) for _ in range(4)]
        m3 = [f32(R3, 512) for _ in range(4)]
        n_st = 0
        for b_i, wsrc in enumerate((w_br_swa, w_br_fox)):
            for c in range(4):
                st, sk = st3[n_st % 2], ("st3", n_st % 2)
                n_st += 1
                S.dma("sp", st[:, 0:D], wsrc[c * 128:(c + 1) * 128, :], W=[sk])
                S.op("dve", V.tensor_copy, wbr[:, b_i, c, :], st[:, 0:D], R=[sk], W=[("wbr", b_i, c)])
        for c in range(8):
            st, sk = st3[n_st % 2], ("st3", n_st % 2)
            n_st += 1
            S.dma("sp", st, w_in[c * 128:(c + 1) * 128, 2312:4360], W=[sk])
            S.op("dve", V.tensor_scalar, wg[:, c, :], st, sm[:, c:c + 1], None, ALU.mult, R=[sk, "sm"], W=[("wg", c)])
        n_u = 0
        for a in range(4):
            ub = a % 2
            S.dma("sp", uT3[ub].rearrange("p c t -> p (c t)"), uT_own_d[a], W=[("uT3", ub)])
            for oc in range(8):
                bs = (n_u % 2) * 4
                n_u += 1
                for b_i, oT in enumerate((oaT, obT)):
                    yk = ("ps", bs + b_i)
                    for c in range(4):
                        S.op("pe", T.matmul, psb[bs + b_i][:], wbr[:, b_i, c, oc * 128:(oc + 1) * 128], oT[:, c, a * 512:(a + 1) * 512],
                             start=(c == 0), stop=(c == 3), R=[("wbr", b_i, c)], W=[yk])
                    gk = ("ps", bs + 2 + b_i)
                    for c in range(8):
                        S.op("pe", T.matmul, psb[bs + 2 + b_i][:], wg[:, c, b_i * D + oc * 128:b_i * D + (oc + 1) * 128], uT3[ub][:, c, :],
                             start=(c == 0), stop=(c == 7), R=[("wg", c), ("uT3", ub)], W=[gk])
                    ti = (n_u * 2 + b_i) % 4
                    S.op("act", A.activation, th3[ti], psb[bs + 2 + b_i][:], AF.Tanh, scale=0.5, R=[gk], W=[("th3", ti)])
                    S.op("dve", V.scalar_tensor_tensor, m3[ti], th3[ti], 1.0, psb[bs + b_i][:], ALU.add, ALU.mult,
                         R=[("th3", ti), yk], W=[("m3", ti)])
                ta, tb = (n_u * 2) % 4, (n_u * 2 + 1) % 4
                S.op("dve", V.tensor_tensor, mixedT[:, oc, a * 512:(a + 1) * 512], m3[ta], m3[tb], ALU.add,
                     R=[("m3", ta), ("m3", tb)], W=[("mixedT", a)])

        if DEBUG == 2:
            S.barrier()
            S.dma("pool", dbg_mx, mixedT.rearrange("p c t -> p (c t)"), sk="dbg")
        S.barrier()
        hres = arena[:, 26 * 512:26 * 512 + 2 * NSLOT * D].bitcast(F32).rearrange("p (m n) -> p m n", m=NSLOT)
        R4 = Region(90, 164)
        wmix = bf(R4, 8 * D).rearrange("p (c n) -> p c n", c=8)
        st4 = [f32(R4, D) for _ in range(2)]
        xin4 = [f32(R4, D) for _ in range(3)]
        for c in range(8):
            st, sk = st4[c % 2], ("st4", c % 2)
            S.dma("sp", st, w_mix[c * 128:(c + 1) * 128, :], W=[sk])
            S.op("dve", V.tensor_scalar, wmix[:, c, :], st, 0.5, None, ALU.mult, R=[sk], W=[("wmix", c)])
        for m in range(NSLOT):
            xi, xk = xin4[m % 3], ("xin4", m % 3)
            S.dma("sp", xi, x_own[m * 128:(m + 1) * 128, :], W=[xk])
            for hf in range(2):
                bank = (2 * m + hf) % 8
                pk = ("ps", bank)
                for c in range(8):
                    S.op("pe", T.matmul, psb[bank][:], mixedT[:, c, m * 128:(m + 1) * 128], wmix[:, c, hf * 512:(hf + 1) * 512],
                         start=(c == 0), stop=(c == 7), R=[("wmix", c)], W=[pk])
                S.op("dve", V.tensor_tensor, hres[:, m, hf * 512:(hf + 1) * 512], psb[bank][:], xi[:, hf * 512:(hf + 1) * 512], ALU.add,
                     R=[pk, xk], W=[("h", m)])

        if DEBUG == 2:
            S.barrier()
            S.dma("pool", dbg_h1, hres.rearrange("p m n -> p (m n)"), sk="dbg")
        S.barrier()
        R5 = Region(90, ARENA_KB)
        u2T = bf(R5, 8 * NOWN).rearrange("p (c t) -> p c t", c=8)
        B5 = NS(xs=[bf(R5, D) for _ in range(2)], junk=[bf(R5, D) for _ in range(2)])
        W1e = [bf(R5, 8 * 512).rearrange("p (c n) -> p c n", c=8) for _ in range(2)]
        W2e = [bf(R5, 4 * D).rearrange("p (k n) -> p k n", k=4) for _ in range(2)]
        st5 = [f32(R5, D) for _ in range(2)]
        hid = [bf(R5, 4 * 512).rearrange("p (k t) -> p k t", k=4) for _ in range(2)]
        rl = [bf(R5, 512) for _ in range(2)]
        pendb = []
        for m in range(NSLOT):
            xs_, xk_ = norm_a(hres[:, m, :], ("h", m), B5, m)
            if pendb:
                norm_b(*pendb.pop())
            pendb.append((xs_, xk_, u2T[:, :, m * 128:(m + 1) * 128], [("u2T", m)], 6 + (m % 2)))
        norm_b(*pendb.pop())
        cnt5 = NS(st=0, b=0)

        def ffn_weights(e):
            eb = e % 2
            for c in range(8):
                st, sk = st5[cnt5.st % 2], ("st5", cnt5.st % 2)
                cnt5.st += 1
                S.dma("sp", st[:, 0:512], w_ff1[c * 128:(c + 1) * 128, e * 512:(e + 1) * 512], W=[sk])
                S.op("dve", V.tensor_scalar, W1e[eb][:, c, :], st[:, 0:512], sm[:, 8 + c:9 + c], None, ALU.mult,
                     R=[sk, "sm"], W=[("W1e", eb, c)])
            for k in range(4):
                st, sk = st5[cnt5.st % 2], ("st5", cnt5.st % 2)
                cnt5.st += 1
                S.dma("sp", st, w_ff2[(e * 4 + k) * 128:(e * 4 + k + 1) * 128, :], W=[sk])
                S.op("dve", V.tensor_copy, W2e[eb][:, k, :], st, R=[sk], W=[("W2e", eb, k)])

        def ffn_l1(i):
            e, a = divmod(i, 4)
            eb, hb = e % 2, i % 2
            for k in range(4):
                bank = cnt5.b % 4
                cnt5.b += 1
                pk = ("ps", bank)
                for c in range(8):
                    S.op("pe", T.matmul, psb[bank][:], W1e[eb][:, c, k * 128:(k + 1) * 128], u2T[:, c, a * 512:(a + 1) * 512],
                         start=(c == 0), stop=(c == 7), R=[("W1e", eb, c)] + [("u2T", 4 * a + t) for t in range(4)], W=[pk])
                r_i = cnt5.b % 2
                S.op("act", A.activation, rl[r_i], psb[bank][:], AF.Relu, R=[pk], W=[("rl", r_i)])
                S.op("pool", G.tensor_tensor, hid[hb][:, k, :], rl[r_i], rl[r_i], ALU.mult, R=[("rl", r_i)], W=[("hid", hb, k)])

        def ffn_l2(i):
            e, a = divmod(i, 4)
            eb, hb = e % 2, i % 2
            for t in range(4):
                m = 4 * a + t
                for hf in range(2):
                    bank = 4 + (cnt5.b % 4)
                    cnt5.b += 1
                    pk = ("ps", bank)
                    for k in range(4):
                        S.op("pe", T.matmul, psb[bank][:], hid[hb][:, k, t * 128:(t + 1) * 128], W2e[eb][:, k, hf * 512:(hf + 1) * 512],
                             start=(k == 0), stop=(k == 3), R=[("hid", hb, k), ("W2e", eb, k)], W=[pk])
                    S.op("dve", V.tensor_tensor, hres[:, m, hf * 512:(hf + 1) * 512], psb[bank][:], hres[:, m, hf * 512:(hf + 1) * 512],
                         ALU.add, R=[pk, ("h", m)], W=[("h", m)])

        for i in range(33):
            if i < 32:
                if i % 4 == 0:
                    ffn_weights(i // 4)
                ffn_l1(i)
            if i >= 1:
                ffn_l2(i - 1)

        if DEBUG == 2:
            S.barrier()
            S.dma("pool", dbg_h2, hres.rearrange("p m n -> p (m n)"), sk="dbg")
        S.barrier()
        R6 = Region(90, ARENA_KB)
        wpg = bf(R6, 8 * D).rearrange("p (c n) -> p c n", c=8)
        wpp = bf(R6, 2 * D).rearrange("p (c n) -> p c n", c=2)
        st6 = [f32(R6, D) for _ in range(2)]
        B6 = NS(xs=[bf(R6, D) for _ in range(2)], junk=[bf(R6, D) for _ in range(2)])
        u3T = [bf(R6, 8 * 128).rearrange("p (c t) -> p c t", c=8) for _ in range(3)]
        pin = [f32(R6, 256) for _ in range(3)]
        pbf = [bf(R6, 256) for _ in range(3)]
        pT = [bf(R6, 256).rearrange("p (c t) -> p c t", c=2) for _ in range(3)]
        th6 = [f32(R6, 512) for _ in range(2)]
        tm6 = [f32(R6, 512) for _ in range(2)]
        outt = [f32(R6, D) for _ in range(2)]
        for c in range(8):
            st, sk = st6[c % 2], ("st6", c % 2)
            S.dma("sp", st, w_pg[c * 128:(c + 1) * 128, :], W=[sk])
            S.op("dve", V.tensor_scalar, wpg[:, c, :], st, sm[:, 16 + c:17 + c], None, ALU.mult, R=[sk, "sm"], W=[("wpg", c)])
        for c in range(2):
            st, sk = st6[c % 2], ("st6", c % 2)
            S.dma("sp", st, w_pp[c * 128:(c + 1) * 128, :], W=[sk])
            S.op("dve", V.tensor_copy, wpp[:, c, :], st, R=[sk], W=[("wpp", c)])
        out_toks = []

        def ple_prep(m):
            mb = m % 3
            norm_tile(hres[:, m, :], ("h", m), B6, m, u3T[mb], [("u3T", mb)], 6)
            S.dma("sp", pin[mb], p_own[m * 128:(m + 1) * 128, :], W=[("pin", mb)])
            S.op("dve", V.tensor_copy, pbf[mb], pin[mb], R=[("pin", mb)], W=[("pbf", mb)])
            pk7 = ("ps", 7)
            pv = psb[7][:].bitcast(BF16)[:, 0:256].rearrange("p (c t) -> p c t", c=2)
            for c in range(2):
                S.op("pe", T.transpose, pv[:, c, :], pbf[mb][:, c * 128:(c + 1) * 128], identb, R=[("pbf", mb), "cbf"], W=[pk7])
            S.op("dve", V.tensor_copy, pT[mb], pv, R=[pk7], W=[("pT", mb)])

        def ple_mm(m):
            mb = m % 3
            for hf in range(2):
                gb, pb = (2 * hf) % 4, (2 * hf + 1) % 4
                gk, ppk = ("ps", gb), ("ps", pb)
                for c in range(8):
                    S.op("pe", T.matmul, psb[gb][:], u3T[mb][:, c, :], wpg[:, c, hf * 512:(hf + 1) * 512], start=(c == 0), stop=(c == 7),
                         R=[("wpg", c), ("u3T", mb)], W=[gk])
                for c in range(2):
                    S.op("pe", T.matmul, psb[pb][:], pT[mb][:, c, :], wpp[:, c, hf * 512:(hf + 1) * 512], start=(c == 0), stop=(c == 1),
                         R=[("wpp", c), ("pT", mb)], W=[ppk])
                S.op("act", A.activation, th6[hf], psb[gb][:], AF.Tanh, scale=0.5, R=[gk], W=[("th6", hf)])
                S.op("dve", V.scalar_tensor_tensor, tm6[hf], th6[hf], 1.0, psb[pb][:], ALU.add, ALU.mult,
                     R=[("th6", hf), ppk], W=[("tm6", hf)])
                S.op("dve", V.scalar_tensor_tensor, hres[:, m, hf * 512:(hf + 1) * 512], tm6[hf], 0.5, hres[:, m, hf * 512:(hf + 1) * 512],
                     ALU.mult, ALU.add, R=[("tm6", hf), ("h", m)], W=[("h", m)])

        def ple_final(m):
            mb = m % 2
            ctr.stat += 1
            sl = ctr.stat % 8
            ss = stat[:, sl * 3:sl * 3 + 1]
            ln = stat[:, sl * 3 + 1:sl * 3 + 2]
            rs = stat[:, sl * 3 + 2:sl * 3 + 3]
            k = ("stat", sl)
            jk = ("junk", 0)
            S.op("act", A.activation, B6.junk[0], hres[:, m, :], AF.Square, accum_out=ss, R=[("h", m)], W=[jk, k])
            S.op("act", A.activation, dummy_t, eps_t, AF.Copy, R=["eps"], W=[k, "dummy"])
            S.op("act", A.activation, ln, ss, AF.Ln, bias=eps_t, scale=1.0 / D, R=[k, "eps"], W=[k])
            S.op("act", A.activation, rs, ln, AF.Exp, scale=-0.5, R=[k], W=[k])
            S.op("dve", V.scalar_tensor_tensor, outt[mb], hres[:, m, :], rs, gfin, ALU.mult, ALU.mult,
                 R=[("h", m), k, "gfin"], W=[("outt", mb)])
            out_toks.append(S.dma("pool", out_d[m * 128:(m + 1) * 128, :], outt[mb], R=[("outt", mb)]))

        ple_prep(0)
        ple_prep(1)
        for m in range(NSLOT):
            ple_mm(m)
            if m + 2 < NSLOT:
                ple_prep(m + 2)
            ple_final(m)
        if DEBUG == 2:
            S.barrier()
            out_toks.append(S.dma("pool", dbg_h3, hres.rearrange("p m n -> p (m n)"), sk="dbg"))
        S.emit(out_toks)
    return nc


_PROG = None


def _host_consts(j):
    ident = np.eye(P, dtype=np.float32)
    tri = (np.arange(P)[:, None] <= np.arange(P)[None, :]).astype(np.float32)
    ones = np.ones((P, P), np.float32)
    maskF = np.zeros((P, 4, P), np.float32)
    for s in range(4):
        if s == j:
            maskF[:, s, :] = np.where(np.arange(P)[:, None] <= np.arange(P)[None, :], 0.0, NEG)
        elif s > j:
            maskF[:, s, :] = NEG
    sw = np.zeros((P, 2, 8, 2, P), np.float32)
    qi = np.arange(P)[None, :] + P
    for t in range(2):
        si = np.arange(P)[:, None] + t * P
        cd = qi // 64 - si // 64
        ok = (cd >= 0) & (cd <= 2)
        for h in range(8):
            slope = 2.0 ** (-(h + 1))
            b = np.where(ok, -slope * np.abs(qi - si), NEG)
            sw[:, 1, h, t, :] = b
            sw[:, 0, h, t, :] = b
    if j == 0:
        sw[:, 0, :, 0, :] = NEG
    cbf = np.concatenate([ident, tri, ones, maskF.reshape(P, -1), sw.reshape(P, -1)], axis=1)
    Wall = (np.arange(64)[:, None] < np.arange(64)[None, :]).astype(np.float32)
    Wb = np.zeros((64, NSLOT, P), np.float32)
    for m in range(NSLOT):
        Wb[:4 * m + j, m, :] = 1.0
    c64 = np.concatenate([Wall, Wb.reshape(64, -1)], axis=1)
    return cbf.astype(ml_dtypes.bfloat16), c64.astype(ml_dtypes.bfloat16)


def _make_in_maps(x, p, g_mix, w_in, b_forget, swa_sinks, w_br_swa, w_br_fox, w_mix_out,
                  g_mlp, w_ff1, w_ff2, g_ple, w_ple_gate, w_ple_proj, g_final):
    f = lambda a: np.ascontiguousarray(np.asarray(a, dtype=np.float32))
    x, p = f(x), f(p)

    def gl(g):
        return f(g).reshape(8, P).T
    smalls = np.concatenate([gl(g_mix[0]), gl(g_mlp[0]), gl(g_ple[0]),
                             np.tile(f(b_forget[0])[None, :], (P, 1)), np.tile(f(swa_sinks[0])[None, :], (P, 1))], axis=1)
    gfin = np.tile(f(g_final)[None, :], (P, 1))
    shared = {"w_in": f(w_in[0]), "w_br_swa": f(w_br_swa[0]), "w_br_fox": f(w_br_fox[0]), "w_mix_out": f(w_mix_out[0]),
              "w_ff1": f(w_ff1[0]), "w_ff2": f(w_ff2[0]), "w_ple_gate": f(w_ple_gate[0]), "w_ple_proj": f(w_ple_proj[0]),
              "smalls": np.ascontiguousarray(smalls), "gfin": np.ascontiguousarray(gfin)}
    maps = []
    for core in range(8):
        b, j = core // 4, core % 4
        xb = x[b].reshape(NB, P, D)
        own = np.ascontiguousarray(xb[j::4]).reshape(NOWN, D)
        halo = np.zeros((NSLOT, P, D), np.float32)
        for m in range(NSLOT):
            g = 4 * m + j - 1
            if g >= 0:
                halo[m] = xb[g]
        pb = p[0, b].reshape(NB, P, 256)
        cbf, c64 = _host_consts(j)
        mp = dict(shared)
        mp.update({"x_all": np.ascontiguousarray(x[b]), "x_own": own, "x_halo": halo.reshape(NOWN, D),
                   "p_own": np.ascontiguousarray(pb[j::4]).reshape(NOWN, 256), "cbf": cbf, "c64": c64})
        maps.append(mp)
    return maps


def kernel(**inputs):
    global _PROG
    if _PROG is None:
        _PROG = build_program()
    maps = _make_in_maps(**inputs)
    res = run_bass_kernel_spmd(_PROG, maps, core_ids=list(range(8)))
    out = np.zeros((2, NB, P, D), np.float32)
    for core in range(8):
        b, j = core // 4, core % 4
        out[b, j::4] = np.asarray(res.results[core]["out"], dtype=np.float32).reshape(NSLOT, P, D)
    return out.reshape(2, S_LEN, D)
```

```python
from contextlib import ExitStack
from functools import partial
import numpy as np
import ml_dtypes
import concourse.bass as bass
import concourse.mybir as mybir
from concourse.bass_utils import run_bass_kernel_spmd

F32 = mybir.dt.float32
BF16 = mybir.dt.bfloat16
AF = mybir.ActivationFunctionType
ALU = mybir.AluOpType

P = 128
D = 1024
S_LEN = 8192
NB = 64
NSLOT = 16
NOWN = NSLOT * P
D_IN = 4360
DFF = 4096
NEG = -30000.0
ROT = 4000
DEBUG = False


class Sched:
    def __init__(self, nc, ctx):
        self.nc = nc
        self.ctx = ctx
        self.eng = {"pe": nc.tensor, "act": nc.scalar, "dve": nc.vector, "pool": nc.gpsimd, "sp": nc.sync}
        self.csem, self.ccnt, self.dsem, self.dcnt = {}, {}, {}, {}
        self.ops = {k: [] for k in self.eng}
        self.last_w, self.readers = {}, {}
        self.nsem = 0
        self.own = {k: set() for k in self.eng}
        for k in self.eng:
            self._new_csem(k)
        self.dk = {}
        self.bar = []

    def _new_csem(self, k):
        s = self.ctx.enter_context(self.nc.semaphore(f"c{self.nsem}_{k}"))
        self.csem[k] = s
        self.own[k].add(id(s))
        self.ccnt[k] = 0
        self.nsem += 1

    def _new_dsem(self, k):
        self.dsem[k] = self.ctx.enter_context(self.nc.semaphore(f"d{self.nsem}_{k}"))
        self.dcnt[k] = 0
        self.nsem += 1

    def _deps(self, reads, writes, extra):
        deps = {}

        def add(tok):
            if tok is None:
                return
            s, v = tok
            if id(s) not in deps or deps[id(s)][1] < v:
                deps[id(s)] = (s, v)

        for k in reads:
            add(self.last_w.get(k))
        for k in writes:
            add(self.last_w.get(k))
            for t in self.readers.get(k, ()):
                add(t)
        for t in extra:
            add(t)
        for t in self.bar:
            add(t)
        return list(deps.values())

    def _commit(self, tok, reads, writes):
        for k in reads:
            self.readers.setdefault(k, []).append(tok)
        for k in writes:
            self.last_w[k] = tok
            self.readers[k] = []

    def op(self, eng, f, *args, R=(), W=(), X=(), **kw):
        deps = self._deps(R, W, X)
        if eng == "pe":
            deps = [d for d in deps if id(d[0]) not in self.own["pe"]]
        if self.ccnt[eng] >= ROT:
            self._new_csem(eng)
        self.ccnt[eng] += 1
        tok = (self.csem[eng], self.ccnt[eng])
        self.ops[eng].append((deps, partial(f, *args, **kw), tok, 1))
        self._commit(tok, R, W)
        return tok

    def dma(self, q, out, in_, R=(), W=(), X=(), sk=None):
        key = sk if sk is not None else (W[0] if W else ("rd", R[0]))
        if key not in self.dk:
            self.dk[key] = [self.ctx.enter_context(self.nc.semaphore(f"d{self.nsem}")), 0]
            self.nsem += 1
        ent = self.dk[key]
        prev = [(ent[0], ent[1])] if ent[1] > 0 else []
        deps = self._deps(R, W, list(X) + prev)
        ent[1] += 16
        tok = (ent[0], ent[1])
        self.ops[q].append((deps, partial(self.eng[q].dma_start, out=out, in_=in_), tok, 16))
        self._commit(tok, R, W)
        return tok

    def barrier(self):
        bar = []
        for k in self.eng:
            if self.ccnt[k] > 0:
                bar.append((self.csem[k], self.ccnt[k]))
        for ent in self.dk.values():
            if ent[1] > 0:
                bar.append((ent[0], ent[1]))
        self.bar = bar
        self.last_w, self.readers = {}, {}

    def emit(self, final_tokens):
        nc = self.nc
        ops = self.ops
        fin = {}
        for s, v in final_tokens:
            if id(s) not in fin or fin[id(s)][1] < v:
                fin[id(s)] = (s, v)

        def run(k, e):
            seen = {}
            for deps, fn, tok, inc in ops[k]:
                for s, v in deps:
                    if seen.get(id(s), 0) >= v:
                        continue
                    e.wait_ge(s, v)
                    seen[id(s)] = v
                fn().then_inc(tok[0], inc)
            if k == "sp":
                for s, v in fin.values():
                    e.wait_ge(s, v)

        with nc.Block() as block:
            @block.sync
            def _(e):
                run("sp", e)

            @block.tensor
            def _(e):
                run("pe", e)

            @block.scalar
            def _(e):
                run("act", e)

            @block.vector
            def _(e):
                run("dve", e)

            @block.gpsimd
            def _(e):
                run("pool", e)


class NS:
    def __init__(self, **kw):
        self.__dict__.update(kw)


ARENA_KB = 207.75
NCONST = 128 * 3 + 512 + 4096
C_ID, C_TRI, C_ONE, C_MF, C_SW = 0, 128, 256, 384, 896


def build_program():
    nc = bass.Bass("TRN2", target_bir_lowering=False)
    V, A, T, G = nc.vector, nc.scalar, nc.tensor, nc.gpsimd

    def din(name, shape, dt=F32):
        return nc.dram_tensor(name, list(shape), dt, kind="ExternalInput").ap()

    x_all = din("x_all", [S_LEN, D])
    x_own = din("x_own", [NOWN, D])
    x_halo = din("x_halo", [NOWN, D])
    p_own = din("p_own", [NOWN, 256])
    w_in = din("w_in", [D, D_IN])
    w_br_swa = din("w_br_swa", [512, D])
    w_br_fox = din("w_br_fox", [512, D])
    w_mix = din("w_mix_out", [D, D])
    w_ff1 = din("w_ff1", [D, DFF])
    w_ff2 = din("w_ff2", [DFF, D])
    w_pg = din("w_ple_gate", [D, D])
    w_pp = din("w_ple_proj", [256, D])
    smalls = din("smalls", [P, 40])
    gfin_d = din("gfin", [P, D])
    cbf_d = din("cbf", [P, NCONST], BF16)
    c64_d = din("c64", [64, 64 + NOWN], BF16)
    out_d = nc.dram_tensor("out", [NOWN, D], F32, kind="ExternalOutput").ap()
    uT_all_d = nc.dram_tensor("uT_all_scr", [16, P, 8 * 512], BF16, kind="Internal").ap()
    uT_own_d = nc.dram_tensor("uT_own_scr", [4, P, 8 * 512], BF16, kind="Internal").ap()
    if DEBUG:
        dbg_oa = nc.dram_tensor("dbg_oa", [P, 4 * NOWN], BF16, kind="ExternalOutput").ap()
        dbg_ob = nc.dram_tensor("dbg_ob", [P, 4 * NOWN], BF16, kind="ExternalOutput").ap()
    if DEBUG == 2:
        dbg_mx = nc.dram_tensor("dbg_mx", [P, 8 * NOWN], BF16, kind="ExternalOutput").ap()
        dbg_h1 = nc.dram_tensor("dbg_h1", [P, NSLOT * D], F32, kind="ExternalOutput").ap()
        dbg_h2 = nc.dram_tensor("dbg_h2", [P, NSLOT * D], F32, kind="ExternalOutput").ap()
        dbg_h3 = nc.dram_tensor("dbg_h3", [P, NSLOT * D], F32, kind="ExternalOutput").ap()

    with ExitStack() as ctx:
        arena = ctx.enter_context(nc.sbuf_tensor("arena", [P, int(ARENA_KB * 512)], BF16))
        psb = [ctx.enter_context(nc.psum_tensor(f"ps{i}", [P, 512], F32)) for i in range(8)]
        S = Sched(nc, ctx)

        class Region:
            def __init__(self, start_kb, end_kb):
                self.off = int(start_kb * 1024)
                self.end = int(end_kb * 1024)

            def take(self, nbytes, parts=P, dt=BF16):
                nbytes = (nbytes + 31) // 32 * 32
                o = self.off
                self.off += nbytes
                assert self.off <= self.end, (self.off, self.end)
                v = arena[0:parts, o // 2:(o + nbytes) // 2]
                if dt == F32:
                    v = v.bitcast(F32)
                return v

        def bf(region, ncols, parts=P):
            return region.take(ncols * 2, parts, BF16)[:, 0:ncols]

        def f32(region, ncols, parts=P):
            return region.take(ncols * 4, parts, F32)[:, 0:ncols]

        RC = Region(0, 26)
        cbf = bf(RC, NCONST)
        c64 = bf(RC, 64 + NOWN, 64)
        sm = f32(RC, 40)
        gfin = f32(RC, D)
        negc = f32(RC, 512)
        onesf = f32(RC, 64)
        sinkexp = f32(RC, 8)
        sinkrow = f32(RC, 1024)
        onesf2 = f32(RC, 128)
        stat = f32(RC, 3 * 8)
        eps_t = f32(RC, 1)
        dummy_t = f32(RC, 1)
        lown_keep = bf(RC, 3 * 128).rearrange("p (i n) -> p i n", i=3)
        identb = cbf[:, C_ID:C_ID + 128]
        triU = cbf[:, C_TRI:C_TRI + 128]
        onesb = cbf[:, C_ONE:C_ONE + 128]
        maskF = cbf[:, C_MF:C_MF + 512].rearrange("p (s q) -> p s q", s=4)
        swab = cbf[:, C_SW:C_SW + 4096].rearrange("p (f h t q) -> p f h t q", f=2, h=8, t=2)
        Wall = c64[:, 0:64]
        Wb = c64[:, 64:64 + NOWN]

        S.dma("sp", cbf, cbf_d, W=["cbf"])
        S.dma("sp", c64, c64_d, W=["c64"])
        S.dma("sp", sm, smalls, W=["sm"])
        S.dma("sp", gfin, gfin_d, W=["gfin"])
        S.op("dve", V.memset, onesf, 1.0, W=["onesf"])
        S.op("dve", V.memset, onesf2, 1.0, W=["onesf2"])
        S.op("dve", V.memset, eps_t, 1e-6, W=["eps"])
        S.op("act", A.activation, sinkexp, sm[:, 32:40], AF.Exp, R=["sm"], W=["sinkexp"])
        for hh_ in range(8):
            S.op("dve", V.tensor_scalar, sinkrow[64:65, hh_ * 128:(hh_ + 1) * 128], onesf2[64:65, :], sinkexp[64:65, hh_:hh_ + 1], None,
                 ALU.mult, R=["sinkexp", "onesf2"], W=["sinkrow"])

        oaT = arena[:, 26 * 512:26 * 512 + 4 * NOWN].rearrange("p (c t) -> p c t", c=4)
        obT = arena[:, 42 * 512:42 * 512 + 4 * NOWN].rearrange("p (c t) -> p c t", c=4)
        Qaug = arena[:, 58 * 512:58 * 512 + 8 * NOWN].rearrange("p (h t) -> p h t", h=8)

        ctr = NS(stat=0, fm=0, tile=0, unit=0)

        def norm_a(src, skey, B, i):
            ctr.stat += 1
            sl = ctr.stat % 8
            ss = stat[:, sl * 3:sl * 3 + 1]
            ln = stat[:, sl * 3 + 1:sl * 3 + 2]
            rs = stat[:, sl * 3 + 2:sl * 3 + 3]
            k = ("stat", sl)
            xs, xkey = B.xs[i % len(B.xs)], ("xs", i % len(B.xs))
            junk, jkey = B.junk[0], ("junk", 0)
            S.op("act", A.activation, junk, src, AF.Square, accum_out=ss, R=[skey], W=[jkey, k])
            S.op("act", A.activation, dummy_t, eps_t, AF.Copy, R=["eps"], W=[k, "dummy"])
            S.op("act", A.activation, ln, ss, AF.Ln, bias=eps_t, scale=1.0 / D, R=[k, "eps"], W=[k])
            S.op("act", A.activation, rs, ln, AF.Exp, scale=-0.5, R=[k], W=[k])
            S.op("dve", V.tensor_scalar, xs, src, rs, None, ALU.mult, R=[skey, k], W=[xkey])
            return xs, xkey

        def norm_b(xs, xkey, dst3, dkeys, pbank):
            pk = ("ps", pbank)
            pv = psb[pbank][:].bitcast(BF16).rearrange("p (c t) -> p c t", c=8)
            for c in range(8):
                S.op("pe", T.transpose, pv[:, c, :], xs[:, c * 128:(c + 1) * 128], identb, R=[xkey, "cbf"], W=[pk])
            S.op("dve", V.tensor_copy, dst3, pv, R=[pk], W=dkeys)

        def norm_tile(src, skey, B, i, dst3, dkeys, pbank):
            xs, xkey = norm_a(src, skey, B, i)
            norm_b(xs, xkey, dst3, dkeys, pbank)

        def stream_chunk(src_d, chunk, B, ub, cache_d=None, cname=None):
            pendb = []
            for t in range(4):
                i = ctr.tile
                ctr.tile += 1
                xi, xk = B.xin[i % len(B.xin)], ("xin", i % len(B.xin))
                r0 = chunk * 512 + t * 128
                S.dma("sp", xi, src_d[r0:r0 + 128, :], W=[xk])
                xs, xkey = norm_a(xi, xk, B, i)
                if pendb:
                    norm_b(*pendb.pop())
                pendb.append((xs, xkey, B.uT[ub][:, :, t * 128:(t + 1) * 128], [("uT", ub, t)], 6 + (i % 2)))
            norm_b(*pendb.pop())
            if cache_d is not None:
                S.dma("pool", cache_d[chunk], B.uT[ub].rearrange("p c t -> p (c t)"), R=[("uT", ub, t) for t in range(4)],
                      W=[("uTd", cname, chunk)], sk=("cw", ub))

        def proj_fm(B, ub, w, wkey, col0, evacs, bank=None):
            if bank is None:
                ctr.fm += 1
                bank = ctr.fm % 2
            pk = ("ps", bank)
            for c in range(8):
                S.op("pe", T.matmul, psb[bank][:], w[:, c, col0:col0 + 128], B.uT[ub][:, c, :], start=(c == 0), stop=(c == 7),
                     R=[(wkey, c)] + [("uT", ub, t) for t in range(4)], W=[pk])
            for (eng, dst, dkey, prow, scale) in evacs:
                src = psb[bank][prow:prow + 64, :]
                if eng == "act":
                    S.op("act", A.mul, dst, src, float(scale), R=[pk], W=[dkey])
                else:
                    S.op("dve", V.tensor_scalar, dst, src, float(scale), None, ALU.mult, R=[pk], W=[dkey])

        def normalize_p1(B, obank, sink_g, dests, idx):
            ok = ("ps", obank)
            rr, rk = B.rrow[idx % 2], ("rrow", idx % 2)
            nt, nk = B.numt[idx % 2], ("numt", idx % 2)
            if sink_g is not None:
                S.op("dve", V.tensor_tensor, rr[64:65, :], psb[obank][64:65, :], sinkrow[64:65, sink_g * 512:(sink_g + 1) * 512],
                     ALU.add, R=[ok, "sinkrow"], W=[rk])
                S.op("act", A.activation, rr[64:65, :], rr[64:65, :], AF.Ln, R=[rk], W=[rk])
                S.op("act", A.activation, rr[64:65, :], rr[64:65, :], AF.Exp, scale=-1.0, R=[rk], W=[rk])
            else:
                S.op("dve", V.reciprocal, rr[64:65, :], psb[obank][64:65, :], R=[ok], W=[rk])
            S.op("dve", V.tensor_copy, nt[0:64, :], psb[obank][0:64, :], R=[ok], W=[nk])

        def normalize_p2(B, obank, sink_g, dests, idx):
            rr, rk = B.rrow[idx % 2], ("rrow", idx % 2)
            nt, nk = B.numt[idx % 2], ("numt", idx % 2)
            bk = ("ps", 5)
            S.op("pe", T.matmul, psb[5][0:64, :], onesf[64:65, 0:64], rr[64:65, :], start=True, stop=True,
                 R=[rk, "onesf"], W=[bk])
            for (dst, vf, dkeys) in dests:
                S.op("dve", V.tensor_tensor, dst, vf(nt[0:64, :]), vf(psb[5][0:64, :]), ALU.mult,
                     R=[nk, bk], W=dkeys)

        def normalize_rows(B, obank, sink_g, dests, idx):
            normalize_p1(B, obank, sink_g, dests, idx)
            normalize_p2(B, obank, sink_g, dests, idx)

        def split3(z, zkey, dst3, dkey, n, parts, B):
            t0, t1 = B.tmpf[0], B.tmpf[1]
            S.op("dve", V.tensor_copy, dst3[0:parts, 0, :], z, R=[zkey], W=[dkey])
            S.op("dve", V.tensor_tensor, t0[0:parts, 0:n], z, dst3[0:parts, 0, :], ALU.subtract, R=[zkey, dkey], W=["t0"])
            S.op("dve", V.tensor_copy, dst3[0:parts, 1, :], t0[0:parts, 0:n], R=["t0"], W=[dkey])
            S.op("dve", V.tensor_tensor, t1[0:parts, 0:n], t0[0:parts, 0:n], dst3[0:parts, 1, :], ALU.subtract,
                 R=["t0", dkey], W=["t1"])
            S.op("dve", V.tensor_copy, dst3[0:parts, 2, :], t1[0:parts, 0:n], R=["t1"], W=[dkey])

        def softplus_split(z, zkey, dst3, dkey, n, parts, B):
            S.op("act", A.activation, z, z, AF.Exp, scale=-1.0, R=[zkey], W=[zkey])
            S.op("act", A.activation, z, z, AF.Ln, bias=1.0, R=[zkey], W=[zkey])
            split3(z, zkey, dst3, dkey, n, parts, B)

        def vproj(B, ub, t, w, wkey, c0, ncol, bank):
            pk = ("ps", bank)
            for c in range(8):
                S.op("pe", T.matmul, psb[bank][:, 0:ncol], B.uT[ub][:, c, t * 128:(t + 1) * 128], w[:, c, c0:c0 + ncol],
                     start=(c == 0), stop=(c == 7), R=[(wkey, c), ("uT", ub, t)], W=[pk])
            return pk

        R1 = Region(90, ARENA_KB)
        wq = bf(R1, 8 * 1288).rearrange("p (c n) -> p c n", c=8)
        wst = [f32(R1, 1280) for _ in range(2)]
        B1 = NS(xin=[f32(R1, D) for _ in range(4)], xs=[bf(R1, D) for _ in range(3)], junk=[bf(R1, D) for _ in range(1)],
                uT=[bf(R1, 8 * 512).rearrange("p (c t) -> p c t", c=8) for _ in range(2)],
                rrow=[f32(R1, 512) for _ in range(2)], numt=[f32(R1, 512) for _ in range(2)],
                tmpf=[f32(R1, 512) for _ in range(2)])
        qaT = bf(R1, 8 * 512, 64).rearrange("p (h t) -> p h t", h=8)
        kaT = bf(R1, 4 * 512, 64).rearrange("p (g s t) -> p g s t", g=2, s=2)
        vaA = bf(R1, 2 * 4 * 2 * 65).rearrange("p (s m g d) -> p s m g d", s=2, m=4, g=2)
        PT1 = [bf(R1, 512) for _ in range(3)]
        zb_own = f32(R1, 128)
        l_own3 = bf(R1, 3 * 128).rearrange("p (i n) -> p i n", i=3)

        for c in range(8):
            st, sk = wst[c % 2], ("wst", c % 2)
            S.dma("sp", st[:, 0:1280], w_in[c * 128:(c + 1) * 128, 0:1280], W=[sk])
            S.op("dve", V.tensor_scalar, wq[:, c, 0:1280], st[:, 0:1280], sm[:, c:c + 1], None, ALU.mult,
                 R=[sk, "sm"], W=[("wq", c)])
            S.dma("sp", st[:, 0:8], w_in[c * 128:(c + 1) * 128, 2304:2312], W=[sk])
            S.op("dve", V.tensor_scalar, wq[:, c, 1280:1288], st[:, 0:8], sm[:, c:c + 1], None, ALU.mult,
                 R=[sk, "sm"], W=[("wq", c)])
        S.op("pool", G.memset, vaA[:, :, :, :, 64:65], 1.0, W=["vaA1"])

        pend = []
        for a in range(4):
            ub = 0
            stream_chunk(x_own, a, B1, ub, uT_own_d, "own")
            for hp in range(4):
                proj_fm(B1, ub, wq, "wq", 768 + hp * 128, [
                    ("act", Qaug[0:64, 2 * hp, a * 512:(a + 1) * 512], ("Qaug", 2 * hp, a), 0, 0.125),
                    ("dve", Qaug[0:64, 2 * hp + 1, a * 512:(a + 1) * 512], ("Qaug", 2 * hp + 1, a), 64, 0.125)])
            for hp in range(4):
                proj_fm(B1, ub, wq, "wq", hp * 128, [
                    ("act", qaT[0:64, 2 * hp, :], ("qaT", 2 * hp), 0, 0.125),
                    ("dve", qaT[0:64, 2 * hp + 1, :], ("qaT", 2 * hp + 1), 64, 0.125)])
            proj_fm(B1, ub, wq, "wq", 512, [
                ("act", kaT[0:64, 0, 1, :], ("kaT", 0, 1), 0, 1.0),
                ("dve", kaT[0:64, 1, 1, :], ("kaT", 1, 1), 64, 1.0)])
            for t in range(4):
                bank = 2 + (t % 2)
                pk = vproj(B1, ub, t, wq, "wq", 640, 128, bank)
                for c in range(8):
                    S.op("pe", T.matmul, psb[bank][:, 128:136], B1.uT[ub][:, c, t * 128:(t + 1) * 128], wq[:, c, 1280:1288],
                         start=(c == 0), stop=(c == 7), R=[("wq", c), ("uT", ub, t)], W=[pk])
                S.op("dve", V.tensor_copy, vaA[:, 1, t, :, 0:64], psb[bank][:, 0:128].rearrange("p (g d) -> p g d", g=2),
                     R=[pk], W=[("vaA", 1, t)])
                m = a * 4 + t
                S.op("dve", V.tensor_tensor, zb_own[:, m * 8:(m + 1) * 8], psb[bank][:, 128:136], sm[:, 24:32], ALU.add,
                     R=[pk, "sm"], W=["zb_own"])
            ub = 1
            stream_chunk(x_halo, a, B1, ub, None)
            proj_fm(B1, ub, wq, "wq", 512, [
                ("act", kaT[0:64, 0, 0, :], ("kaT", 0, 0), 0, 1.0),
                ("dve", kaT[0:64, 1, 0, :], ("kaT", 1, 0), 64, 1.0)])
            for t in range(4):
                bank = 2 + (t % 2)
                pk = vproj(B1, ub, t, wq, "wq", 640, 128, bank)
                S.op("dve", V.tensor_copy, vaA[:, 0, t, :, 0:64], psb[bank][:, 0:128].rearrange("p (g d) -> p g d", g=2),
                     R=[pk], W=[("vaA", 0, t)])
            for s in range(4):
                m = a * 4 + s
                first = 0 if m == 0 else 1
                for g in range(2):
                    u = ctr.unit
                    ctr.unit += 1
                    obank = 4 if (u % 2 == 0) else 1
                    ok = ("ps", obank)
                    for t in range(2):
                        sbank = 2 + t
                        sk = ("ps", sbank)
                        sv = psb[sbank][:].rearrange("p (h q) -> p h q", h=4)
                        S.op("pe", T.matmul, sv, kaT[0:64, g, t, s * 128:(s + 1) * 128],
                             qaT[0:64, 4 * g:4 * g + 4, s * 128:(s + 1) * 128], start=True, stop=False,
                             R=[("kaT", g, t)] + [("qaT", 4 * g + i) for i in range(4)], W=[sk])
                        S.op("pe", T.matmul, sv, identb, swab[:, first, 4 * g:4 * g + 4, t, :], start=False, stop=True,
                             R=["cbf"], W=[sk])
                    for t in range(2):
                        sbank = 2 + t
                        sk = ("ps", sbank)
                        pt, ptk = PT1[(2 * u + t) % 3], ("PT", (2 * u + t) % 3)
                        S.op("act", A.activation, pt, psb[sbank][:], AF.Exp, R=[sk], W=[ptk])
                        S.op("pe", T.matmul, psb[obank][0:65, :], vaA[:, t, s, g, :], pt, start=(t == 0), stop=(t == 1),
                             R=[ptk, ("vaA", t, s), "vaA1"], W=[ok])
                    dests = []
                    for par in range(2):
                        dests.append((oaT[par * 64:par * 64 + 64, 2 * g:2 * g + 2, m * 128:(m + 1) * 128],
                                      (lambda v, par=par: v.rearrange("p (c e q) -> p c e q", c=2, e=2)[:, :, par, :]),
                                      [("oaT", 4 * g + par, m), ("oaT", 4 * g + 2 + par, m)]))
                    if pend:
                        normalize_rows(*pend.pop())
                    pend.append((B1, obank, g, dests, u))
            if pend:
                normalize_rows(*pend.pop())

        softplus_split(zb_own, "zb_own", l_own3, "l_own3", 128, P, B1)
        S.op("dve", V.tensor_copy, lown_keep, l_own3, R=["l_own3"], W=["lown_keep"])
        if DEBUG:
            S.dma("pool", dbg_oa, oaT.rearrange("p c t -> p (c t)"), R=[("oaT", h, m) for h in range(8) for m in range(16)])

        S.barrier()
        R2 = Region(90, ARENA_KB)
        wkv = bf(R2, 8 * 264).rearrange("p (c n) -> p c n", c=8)
        wst2 = [f32(R2, 264) for _ in range(2)]
        kaug = bf(R2, 2 * S_LEN).rearrange("p (h t) -> p h t", h=2)
        vaug = bf(R2, NB * 2 * 65).rearrange("p (b h d) -> p b h d", b=NB, h=2)
        B2 = NS(xin=[f32(R2, D) for _ in range(4)], xs=[bf(R2, D) for _ in range(2)], junk=[bf(R2, D) for _ in range(1)],
                uT=[bf(R2, 8 * 512).rearrange("p (c t) -> p c t", c=8) for _ in range(2)],
                rrow=[f32(R2, 512) for _ in range(2)], numt=[f32(R2, 512) for _ in range(2)])
        B2.tmpf = B2.numt
        PT2 = [bf(R2, 512) for _ in range(4)]
        zb = f32(R2, 512)
        l3 = bf(R2, 3 * 512).rearrange("p (i n) -> p i n", i=3)
        totT = f32(R2, 8, 64)
        tot3 = bf(R2, 3 * 8, 64).rearrange("p (i h) -> p i h", i=3)
        RT3 = bf(R2, 3 * 512, 64).rearrange("p (i n) -> p i n", i=3)
        crow = bf(R2, NOWN, 8)

        S.op("dve", V.memset, kaug[64:65, :, :], 1.0, W=["kaug1"])
        S.op("pool", G.memset, vaug[:, :, :, 64:65], 1.0, W=["vaug1"])

        ctr.sc = 0

        def fox_weights(hp):
            kc0 = 1280 + hp * 128
            vc0 = 1792 + hp * 128
            for c in range(8):
                st, sk = wst2[c % 2], ("wst2", c % 2)
                for (d0, s0, n) in ((0, kc0, 128), (128, vc0, 128), (256, 2304, 8)):
                    S.dma("sp", st[:, d0:d0 + n], w_in[c * 128:(c + 1) * 128, s0:s0 + n], W=[sk])
                S.op("dve", V.tensor_scalar, wkv[:, c, :], st[:, 0:264], sm[:, c:c + 1], None, ALU.mult,
                     R=[sk, "sm"], W=[("wkv", c)])

        def fox_stream_chunk(hp, ch, overl):
            ub = ctr.sc % 2
            ctr.sc += 1
            if hp == 0:
                stream_chunk(x_all, ch, B2, ub, uT_all_d, "all")
            else:
                S.dma("sp", B2.uT[ub].rearrange("p c t -> p (c t)"), uT_all_d[ch], R=[("uTd", "all", ch)],
                      W=[("uT", ub, t) for t in range(4)])
            proj_fm(B2, ub, wkv, "wkv", 0, [
                ("dve" if overl else "act", kaug[0:64, 0, ch * 512:(ch + 1) * 512], ("kaug", 0, ch), 0, 1.0),
                ("dve", kaug[0:64, 1, ch * 512:(ch + 1) * 512], ("kaug", 1, ch), 64, 1.0)], bank=(6 if overl else None))
            for t in range(4):
                bank = (7 - (t % 2)) if overl else (2 + (t % 2))
                blk = ch * 4 + t
                ncol = 136 if hp == 0 else 128
                pk = vproj(B2, ub, t, wkv, "wkv", 128, ncol, bank)
                S.op("dve", V.tensor_copy, vaug[:, blk, :, 0:64], psb[bank][:, 0:128].rearrange("p (g d) -> p g d", g=2),
                     R=[pk], W=[("vaug", blk)])
                if hp == 0:
                    S.op("dve", V.tensor_tensor, zb[:, blk * 8:(blk + 1) * 8], psb[bank][:, 128:136], sm[:, 24:32], ALU.add,
                         R=[pk, "sm"], W=["zb"])

        class NextStream:
            def __init__(self, hp):
                self.hp, self.q, self.started = hp, [], False

            def release(self, quarter):
                self.q.extend(range(4 * quarter, 4 * quarter + 4))

            def maybe(self):
                if self.q:
                    if not self.started:
                        fox_weights(self.hp)
                        self.started = True
                    fox_stream_chunk(self.hp, self.q.pop(0), True)

            def flush(self):
                while self.q:
                    self.maybe()

        def fox_decay():
            softplus_split(zb, "zb", l3, "l3", 512, P, B2)
            tk = ("ps", 6)
            l3v = l3.rearrange("p i (b h) -> p i b h", h=8)
            for h in range(8):
                for i in range(3):
                    S.op("pe", T.matmul, psb[6][0:64, h:h + 1], l3v[:, i, :, h], onesb[:, 0:1], start=(i == 0), stop=(i == 2),
                         R=["l3", "cbf"], W=[tk])
            S.op("dve", V.tensor_copy, totT, psb[6][0:64, 0:8], R=[tk], W=["totT"])
            split3(totT, "totT", tot3, "tot3", 8, 64, B2)
            RT3v = RT3.rearrange("p i (b h) -> p i b h", h=8)
            for i in range(3):
                for h in range(8):
                    S.op("dve", V.tensor_scalar, RT3v[:, i, :, h], Wall, tot3[:, i, h:h + 1], None, ALU.mult,
                         R=["c64", "tot3"], W=["RT3"])
            ck = ("ps", 7)
            for i in range(3):
                S.op("pe", T.matmul, psb[7][:], triU, l3[:, i, :], start=(i == 0), stop=False, R=["l3", "cbf"], W=[ck])
            for i in range(3):
                S.op("pe", T.matmul, psb[7][:], onesb[0:64, :], RT3[:, i, :], start=False, stop=(i == 2),
                     R=["RT3", "cbf"], W=[ck])
            S.op("dve", V.tensor_copy, negc, psb[7][:], R=[ck], W=["negc"])
            lov = lown_keep.rearrange("p i (m h) -> p i m h", h=8)
            for q4 in range(4):
                bk = ("ps", 6)
                for mm in range(4):
                    m = q4 * 4 + mm
                    for i in range(3):
                        S.op("pe", T.matmul, psb[6][0:8, mm * 128:(mm + 1) * 128], lov[:, i, m, :], triU,
                             start=(i == 0), stop=False, R=["lown_keep", "cbf"], W=[bk])
                    for i in range(3):
                        S.op("pe", T.matmul, psb[6][0:8, mm * 128:(mm + 1) * 128], tot3[:, i, :], Wb[:, m * 128:(m + 1) * 128],
                             start=False, stop=(i == 2), R=["tot3", "c64"], W=[bk])
                S.op("dve", V.tensor_scalar, crow[:, q4 * 512:(q4 + 1) * 512], psb[6][0:8, :], -1.0, None, ALU.mult,
                     R=[bk], W=["crow"])
            for h in range(8):
                S.dma("pool", Qaug[64:65, h, :], crow[h:h + 1, :], R=["crow"], W=[("Qrow", h)])


        def fox_attention(hp, nxt):
            it = 0
            for a in (3, 2, 1, 0):
                for hh in range(2):
                    h = 2 * hp + hh
                    u = ctr.unit
                    ctr.unit += 1
                    obank = 3 + (u % 2)
                    ok = ("ps", obank)
                    kbs = [(kb, 0, None) for kb in range(16 * a)]
                    for r in range(4):
                        for s_ in range(4):
                            kbs.append((16 * a + 4 * r + s_, r * 128, s_))
                    qreads = [("Qaug", h, a), ("Qrow", h)]

                    def emit_s(n_i):
                        kb, c0, ms = kbs[n_i]
                        ncl = 512 - c0
                        sbank = n_i % 3
                        sk = ("ps", sbank)
                        S.op("pe", T.matmul, psb[sbank][:, 0:ncl], kaug[0:65, hh, kb * 128:(kb + 1) * 128],
                             Qaug[0:65, h, a * 512 + c0:(a + 1) * 512], start=True, stop=(ms is None),
                             R=[("kaug", hh, kb // 4), "kaug1"] + qreads, W=[sk])
                        if ms is not None:
                            S.op("pe", T.matmul, psb[sbank][:, 0:128], identb, maskF[:, ms, :], start=False, stop=True,
                                 R=["cbf"], W=[sk])

                    def emit_rest(n_i):
                        kb, c0, ms = kbs[n_i]
                        ncl = 512 - c0
                        sbank = n_i % 3
                        sk = ("ps", sbank)
                        pt, ptk = PT2[n_i % 4], ("PT", n_i % 4)
                        S.op("act", A.activation, pt[:, 0:ncl], psb[sbank][:, 0:ncl], AF.Exp,
                             bias=negc[:, kb * 8 + h:kb * 8 + h + 1], R=[sk, "negc"], W=[ptk])
                        S.op("pe", T.matmul, psb[obank][0:65, c0:512], vaug[:, kb, hh, :], pt[:, 0:ncl],
                             start=(n_i == 0), stop=(n_i == len(kbs) - 1), R=[ptk, ("vaug", kb), "vaug1"], W=[ok])

                    for n_i in range(len(kbs) + 2):
                        if n_i < len(kbs):
                            emit_s(n_i)
                        if n_i >= 2:
                            emit_rest(n_i - 2)
                        it += 1
                        if nxt is not None and it % 8 == 0:
                            nxt.maybe()
                        if pend and n_i == 3:
                            normalize_p1(*pend[0])
                        if pend and n_i == 12:
                            normalize_p2(*pend.pop())
                    pend.append((B2, obank, None,
                                 [(obT[hh * 64:hh * 64 + 64, hp, a * 512:(a + 1) * 512], (lambda v: v), [("obT", h, a)])], u))
                if nxt is not None:
                    nxt.release(a)
            if pend:
                normalize_rows(*pend.pop())
            if nxt is not None:
                nxt.flush()

        fox_weights(0)
        for ch in range(16):
            fox_stream_chunk(0, ch, False)
        fox_decay()
        for hp in range(4):
            fox_attention(hp, NextStream(hp + 1) if hp < 3 else None)

        if DEBUG == 2:
            S.dma("pool", dbg_ob, obT.rearrange("p c t -> p (c t)"), R=[("obT", h, a) for h in range(8) for a in range(4)])
        if DEBUG == 1:
            tk1 = S.dma("pool", dbg_ob, obT.rearrange("p c t -> p (c t)"), R=[("obT", h, a) for h in range(8) for a in range(4)])
            tk2 = S.dma("pool", out_d[0:128, :], gfin, R=["gfin"])
            S.emit([tk1, tk2])
            return nc

        S.barrier()
        mixedT = arena[:, 164 * 512:164 * 512 + 8 * NOWN].rearrange("p (c t) -> p c t", c=8)
        R3 = Region(58, 164)
        wbr = bf(R3, 2 * 4 * D).rearrange("p (b c n) -> p b c n", b=2, c=4)
        wg = bf(R3, 8 * 2048).rearrange("p (c n) -> p c n", c=8)
        st3 = [f32(R3, 2048) for _ in range(2)]
        uT3 = [bf(R3, 8 * 512).rearrange("p (c t) -> p c t", c=8) for _ in range(2)]
        th3 = [f32(R3, 512) for _ in range(4)]
        m3 = [f32(R3, 512) for _ in range(4)]
        n_st = 0
        for b_i, wsrc in enumerate((w_br_swa, w_br_fox)):
            for c in range(4):
                st, sk = st3[n_st % 2], ("st3", n_st % 2)
                n_st += 1
                S.dma("sp", st[:, 0:D], wsrc[c * 128:(c + 1) * 128, :], W=[sk])
                S.op("dve", V.tensor_copy, wbr[:, b_i, c, :], st[:, 0:D], R=[sk], W=[("wbr", b_i, c)])
        for c in range(8):
            st, sk = st3[n_st % 2], ("st3", n_st % 2)
            n_st += 1
            S.dma("sp", st, w_in[c * 128:(c + 1) * 128, 2312:4360], W=[sk])
            S.op("dve", V.tensor_scalar, wg[:, c, :], st, sm[:, c:c + 1], None, ALU.mult, R=[sk, "sm"], W=[("wg", c)])
        n_u = 0
        for a in range(4):
            ub = a % 2
            S.dma("sp", uT3[ub].rearrange("p c t -> p (c t)"), uT_own_d[a], W=[("uT3", ub)])
            for oc in range(8):
                bs = (n_u % 2) * 4
                n_u += 1
                for b_i, oT in enumerate((oaT, obT)):
                    yk = ("ps", bs + b_i)
                    for c in range(4):
                        S.op("pe", T.matmul, psb[bs + b_i][:], wbr[:, b_i, c, oc * 128:(oc + 1) * 128], oT[:, c, a * 512:(a + 1) * 512],
                             start=(c == 0), stop=(c == 3), R=[("wbr", b_i, c)], W=[yk])
                    gk = ("ps", bs + 2 + b_i)
                    for c in range(8):
                        S.op("pe", T.matmul, psb[bs + 2 + b_i][:], wg[:, c, b_i * D + oc * 128:b_i * D + (oc + 1) * 128], uT3[ub][:, c, :],
                             start=(c == 0), stop=(c == 7), R=[("wg", c), ("uT3", ub)], W=[gk])
                    ti = (n_u * 2 + b_i) % 4
                    S.op("act", A.activation, th3[ti], psb[bs + 2 + b_i][:], AF.Tanh, scale=0.5, R=[gk], W=[("th3", ti)])
                    S.op("dve", V.scalar_tensor_tensor, m3[ti], th3[ti], 1.0, psb[bs + b_i][:], ALU.add, ALU.mult,
                         R=[("th3", ti), yk], W=[("m3", ti)])
                ta, tb = (n_u * 2) % 4, (n_u * 2 + 1) % 4
                S.op("dve", V.tensor_tensor, mixedT[:, oc, a * 512:(a + 1) * 512], m3[ta], m3[tb], ALU.add,
                     R=[("m3", ta), ("m3", tb)], W=[("mixedT", a)])

        if DEBUG == 2:
            S.barrier()
            S.dma("pool", dbg_mx, mixedT.rearrange("p c t -> p (c t)"), sk="dbg")
        S.barrier()
        hres = arena[:, 26 * 512:26 * 512 + 2 * NSLOT * D].bitcast(F32).rearrange("p (m n) -> p m n", m=NSLOT)
        R4 = Region(90, 164)
        wmix = bf(R4, 8 * D).rearrange("p (c n) -> p c n", c=8)
        st4 = [f32(R4, D) for _ in range(2)]
        xin4 = [f32(R4, D) for _ in range(3)]
        for c in range(8):
            st, sk = st4[c % 2], ("st4", c % 2)
            S.dma("sp", st, w_mix[c * 128:(c + 1) * 128, :], W=[sk])
            S.op("dve", V.tensor_scalar, wmix[:, c, :], st, 0.5, None, ALU.mult, R=[sk], W=[("wmix", c)])
        for m in range(NSLOT):
            xi, xk = xin4[m % 3], ("xin4", m % 3)
            S.dma("sp", xi, x_own[m * 128:(m + 1) * 128, :], W=[xk])
            for hf in range(2):
                bank = (2 * m + hf) % 8
                pk = ("ps", bank)
                for c in range(8):
                    S.op("pe", T.matmul, psb[bank][:], mixedT[:, c, m * 128:(m + 1) * 128], wmix[:, c, hf * 512:(hf + 1) * 512],
                         start=(c == 0), stop=(c == 7), R=[("wmix", c)], W=[pk])
                S.op("dve", V.tensor_tensor, hres[:, m, hf * 512:(hf + 1) * 512], psb[bank][:], xi[:, hf * 512:(hf + 1) * 512], ALU.add,
                     R=[pk, xk], W=[("h", m)])

        if DEBUG == 2:
            S.barrier()
            S.dma("pool", dbg_h1, hres.rearrange("p m n -> p (m n)"), sk="dbg")
        S.barrier()
        R5 = Region(90, ARENA_KB)
        u2T = bf(R5, 8 * NOWN).rearrange("p (c t) -> p c t", c=8)
        B5 = NS(xs=[bf(R5, D) for _ in range(2)], junk=[bf(R5, D) for _ in range(2)])
        W1e = [bf(R5, 8 * 512).rearrange("p (c n) -> p c n", c=8) for _ in range(2)]
        W2e = [bf(R5, 4 * D).rearrange("p (k n) -> p k n", k=4) for _ in range(2)]
        st5 = [f32(R5, D) for _ in range(2)]
        hid = [bf(R5, 4 * 512).rearrange("p (k t) -> p k t", k=4) for _ in range(2)]
        rl = [bf(R5, 512) for _ in range(2)]
        pendb = []
        for m in range(NSLOT):
            xs_, xk_ = norm_a(hres[:, m, :], ("h", m), B5, m)
            if pendb:
                norm_b(*pendb.pop())
            pendb.append((xs_, xk_, u2T[:, :, m * 128:(m + 1) * 128], [("u2T", m)], 6 + (m % 2)))
        norm_b(*pendb.pop())
        cnt5 = NS(st=0, b=0)

        def ffn_weights(e):
            eb = e % 2
            for c in range(8):
                st, sk = st5[cnt5.st % 2], ("st5", cnt5.st % 2)
                cnt5.st += 1
                S.dma("sp", st[:, 0:512], w_ff1[c * 128:(c + 1) * 128, e * 512:(e + 1) * 512], W=[sk])
                S.op("dve", V.tensor_scalar, W1e[eb][:, c, :], st[:, 0:512], sm[:, 8 + c:9 + c], None, ALU.mult,
                     R=[sk, "sm"], W=[("W1e", eb, c)])
            for k in range(4):
                st, sk = st5[cnt5.st % 2], ("st5", cnt5.st % 2)
                cnt5.st += 1
                S.dma("sp", st, w_ff2[(e * 4 + k) * 128:(e * 4 + k + 1) * 128, :], W=[sk])
                S.op("dve", V.tensor_copy, W2e[eb][:, k, :], st, R=[sk], W=[("W2e", eb, k)])

        def ffn_l1(i):
            e, a = divmod(i, 4)
            eb, hb = e % 2, i % 2
            for k in range(4):
                bank = cnt5.b % 4
                cnt5.b += 1
                pk = ("ps", bank)
                for c in range(8):
                    S.op("pe", T.matmul, psb[bank][:], W1e[eb][:, c, k * 128:(k + 1) * 128], u2T[:, c, a * 512:(a + 1) * 512],
                         start=(c == 0), stop=(c == 7), R=[("W1e", eb, c)] + [("u2T", 4 * a + t) for t in range(4)], W=[pk])
                r_i = cnt5.b % 2
                S.op("act", A.activation, rl[r_i], psb[bank][:], AF.Relu, R=[pk], W=[("rl", r_i)])
                S.op("pool", G.tensor_tensor, hid[hb][:, k, :], rl[r_i], rl[r_i], ALU.mult, R=[("rl", r_i)], W=[("hid", hb, k)])

        def ffn_l2(i):
            e, a = divmod(i, 4)
            eb, hb = e % 2, i % 2
            for t in range(4):
                m = 4 * a + t
                for hf in range(2):
                    bank = 4 + (cnt5.b % 4)
                    cnt5.b += 1
                    pk = ("ps", bank)
                    for k in range(4):
                        S.op("pe", T.matmul, psb[bank][:], hid[hb][:, k, t * 128:(t + 1) * 128], W2e[eb][:, k, hf * 512:(hf + 1) * 512],
                             start=(k == 0), stop=(k == 3), R=[("hid", hb, k), ("W2e", eb, k)], W=[pk])
                    S.op("dve", V.tensor_tensor, hres[:, m, hf * 512:(hf + 1) * 512], psb[bank][:], hres[:, m, hf * 512:(hf + 1) * 512],
                         ALU.add, R=[pk, ("h", m)], W=[("h", m)])

        for i in range(33):
            if i < 32:
                if i % 4 == 0:
                    ffn_weights(i // 4)
                ffn_l1(i)
            if i >= 1:
                ffn_l2(i - 1)

        if DEBUG == 2:
            S.barrier()
            S.dma("pool", dbg_h2, hres.rearrange("p m n -> p (m n)"), sk="dbg")
        S.barrier()
        R6 = Region(90, ARENA_KB)
        wpg = bf(R6, 8 * D).rearrange("p (c n) -> p c n", c=8)
        wpp = bf(R6, 2 * D).rearrange("p (c n) -> p c n", c=2)
        st6 = [f32(R6, D) for _ in range(2)]
        B6 = NS(xs=[bf(R6, D) for _ in range(2)], junk=[bf(R6, D) for _ in range(2)])
        u3T = [bf(R6, 8 * 128).rearrange("p (c t) -> p c t", c=8) for _ in range(4)]
        pin = [f32(R6, 256) for _ in range(4)]
        pbf = [bf(R6, 256) for _ in range(4)]
        pT = [bf(R6, 256).rearrange("p (c t) -> p c t", c=2) for _ in range(4)]
        th6 = [f32(R6, 512) for _ in range(4)]
        tm6 = [f32(R6, 512) for _ in range(4)]
        outt = [f32(R6, D) for _ in range(2)]
        for c in range(8):
            st, sk = st6[c % 2], ("st6", c % 2)
            S.dma("sp", st, w_pg[c * 128:(c + 1) * 128, :], W=[sk])
            S.op("dve", V.tensor_scalar, wpg[:, c, :], st, sm[:, 16 + c:17 + c], None, ALU.mult, R=[sk, "sm"], W=[("wpg", c)])
        for c in range(2):
            st, sk = st6[c % 2], ("st6", c % 2)
            S.dma("sp", st, w_pp[c * 128:(c + 1) * 128, :], W=[sk])
            S.op("dve", V.tensor_copy, wpp[:, c, :], st, R=[sk], W=[("wpp", c)])
        out_toks = []

        def ple_prep(m):
            mb = m % 4
            norm_tile(hres[:, m, :], ("h", m), B6, m, u3T[mb], [("u3T", mb)], 6)
            S.dma("sp", pin[mb], p_own[m * 128:(m + 1) * 128, :], W=[("pin", mb)])
            S.op("dve", V.tensor_copy, pbf[mb], pin[mb], R=[("pin", mb)], W=[("pbf", mb)])
            pk7 = ("ps", 7)
            pv = psb[7][:].bitcast(BF16)[:, 0:256].rearrange("p (c t) -> p c t", c=2)
            for c in range(2):
                S.op("pe", T.transpose, pv[:, c, :], pbf[mb][:, c * 128:(c + 1) * 128], identb, R=[("pbf", mb), "cbf"], W=[pk7])
            S.op("dve", V.tensor_copy, pT[mb], pv, R=[pk7], W=[("pT", mb)])

        def ple_mm(m):
            mb = m % 4
            for hf in range(2):
                gb, pb = (2 * hf) % 4, (2 * hf + 1) % 4
                gk, ppk = ("ps", gb), ("ps", pb)
                ti = (2 * m + hf) % 4
                for c in range(8):
                    S.op("pe", T.matmul, psb[gb][:], u3T[mb][:, c, :], wpg[:, c, hf * 512:(hf + 1) * 512], start=(c == 0), stop=(c == 7),
                         R=[("wpg", c), ("u3T", mb)], W=[gk])
                for c in range(2):
                    S.op("pe", T.matmul, psb[pb][:], pT[mb][:, c, :], wpp[:, c, hf * 512:(hf + 1) * 512], start=(c == 0), stop=(c == 1),
                         R=[("wpp", c), ("pT", mb)], W=[ppk])
                S.op("act", A.activation, th6[ti], psb[gb][:], AF.Tanh, scale=0.5, R=[gk], W=[("th6", ti)])
                S.op("dve", V.scalar_tensor_tensor, tm6[ti], th6[ti], 1.0, psb[pb][:], ALU.add, ALU.mult,
                     R=[("th6", ti), ppk], W=[("tm6", ti)])
                S.op("dve", V.scalar_tensor_tensor, hres[:, m, hf * 512:(hf + 1) * 512], tm6[ti], 0.5, hres[:, m, hf * 512:(hf + 1) * 512],
                     ALU.mult, ALU.add, R=[("tm6", ti), ("h", m)], W=[("h", m)])

        def ple_final(m):
            mb = m % 2
            ctr.stat += 1
            sl = ctr.stat % 8
            ss = stat[:, sl * 3:sl * 3 + 1]
            ln = stat[:, sl * 3 + 1:sl * 3 + 2]
            rs = stat[:, sl * 3 + 2:sl * 3 + 3]
            k = ("stat", sl)
            jk = ("junk", 0)
            S.op("act", A.activation, B6.junk[0], hres[:, m, :], AF.Square, accum_out=ss, R=[("h", m)], W=[jk, k])
            S.op("act", A.activation, dummy_t, eps_t, AF.Copy, R=["eps"], W=[k, "dummy"])
            S.op("act", A.activation, ln, ss, AF.Ln, bias=eps_t, scale=1.0 / D, R=[k, "eps"], W=[k])
            S.op("act", A.activation, rs, ln, AF.Exp, scale=-0.5, R=[k], W=[k])
            S.op("dve", V.scalar_tensor_tensor, outt[mb], hres[:, m, :], rs, gfin, ALU.mult, ALU.mult,
                 R=[("h", m), k, "gfin"], W=[("outt", mb)])
            out_toks.append(S.dma("pool", out_d[m * 128:(m + 1) * 128, :], outt[mb], R=[("outt", mb)]))

        ple_prep(0)
        ple_prep(1)
        ple_prep(2)
        ple_mm(0)
        for m in range(NSLOT):
            if m + 1 < NSLOT:
                ple_mm(m + 1)
            if m + 3 < NSLOT:
                ple_prep(m + 3)
            ple_final(m)
        if DEBUG == 2:
            S.barrier()
            out_toks.append(S.dma("pool", dbg_h3, hres.rearrange("p m n -> p (m n)"), sk="dbg"))
        S.emit(out_toks)
    return nc


_PROG = None


def _host_consts(j):
    ident = np.eye(P, dtype=np.float32)
    tri = (np.arange(P)[:, None] <= np.arange(P)[None, :]).astype(np.float32)
    ones = np.ones((P, P), np.float32)
    maskF = np.zeros((P, 4, P), np.float32)
    for s in range(4):
        if s == j:
            maskF[:, s, :] = np.where(np.arange(P)[:, None] <= np.arange(P)[None, :], 0.0, NEG)
        elif s > j:
            maskF[:, s, :] = NEG
    sw = np.zeros((P, 2, 8, 2, P), np.float32)
    qi = np.arange(P)[None, :] + P
    for t in range(2):
        si = np.arange(P)[:, None] + t * P
        cd = qi // 64 - si // 64
        ok = (cd >= 0) & (cd <= 2)
        for h in range(8):
            slope = 2.0 ** (-(h + 1))
            b = np.where(ok, -slope * np.abs(qi - si), NEG)
            sw[:, 1, h, t, :] = b
            sw[:, 0, h, t, :] = b
    if j == 0:
        sw[:, 0, :, 0, :] = NEG
    cbf = np.concatenate([ident, tri, ones, maskF.reshape(P, -1), sw.reshape(P, -1)], axis=1)
    Wall = (np.arange(64)[:, None] < np.arange(64)[None, :]).astype(np.float32)
    Wb = np.zeros((64, NSLOT, P), np.float32)
    for m in range(NSLOT):
        Wb[:4 * m + j, m, :] = 1.0
    c64 = np.concatenate([Wall, Wb.reshape(64, -1)], axis=1)
    return cbf.astype(ml_dtypes.bfloat16), c64.astype(ml_dtypes.bfloat16)


def _make_in_maps(x, p, g_mix, w_in, b_forget, swa_sinks, w_br_swa, w_br_fox, w_mix_out,
                  g_mlp, w_ff1, w_ff2, g_ple, w_ple_gate, w_ple_proj, g_final):
    f = lambda a: np.ascontiguousarray(np.asarray(a, dtype=np.float32))
    x, p = f(x), f(p)

    def gl(g):
        return f(g).reshape(8, P).T
    smalls = np.concatenate([gl(g_mix[0]), gl(g_mlp[0]), gl(g_ple[0]),
                             np.tile(f(b_forget[0])[None, :], (P, 1)), np.tile(f(swa_sinks[0])[None, :], (P, 1))], axis=1)
    gfin = np.tile(f(g_final)[None, :], (P, 1))
    shared = {"w_in": f(w_in[0]), "w_br_swa": f(w_br_swa[0]), "w_br_fox": f(w_br_fox[0]), "w_mix_out": f(w_mix_out[0]),
              "w_ff1": f(w_ff1[0]), "w_ff2": f(w_ff2[0]), "w_ple_gate": f(w_ple_gate[0]), "w_ple_proj": f(w_ple_proj[0]),
              "smalls": np.ascontiguousarray(smalls), "gfin": np.ascontiguousarray(gfin)}
    maps = []
    for core in range(8):
        b, j = core // 4, core % 4
        xb = x[b].reshape(NB, P, D)
        own = np.ascontiguousarray(xb[j::4]).reshape(NOWN, D)
        halo = np.zeros((NSLOT, P, D), np.float32)
        for m in range(NSLOT):
            g = 4 * m + j - 1
            if g >= 0:
                halo[m] = xb[g]
        pb = p[0, b].reshape(NB, P, 256)
        cbf, c64 = _host_consts(j)
        mp = dict(shared)
        mp.update({"x_all": np.ascontiguousarray(x[b]), "x_own": own, "x_halo": halo.reshape(NOWN, D),
                   "p_own": np.ascontiguousarray(pb[j::4]).reshape(NOWN, 256), "cbf": cbf, "c64": c64})
        maps.append(mp)
    return maps


def kernel(**inputs):
    global _PROG
    if _PROG is None:
        _PROG = build_program()
    maps = _make_in_maps(**inputs)
    res = run_bass_kernel_spmd(_PROG, maps, core_ids=list(range(8)))
    out = np.zeros((2, NB, P, D), np.float32)
    for core in range(8):
        b, j = core // 4, core % 4
        out[b, j::4] = np.asarray(res.results[core]["out"], dtype=np.float32).reshape(NSLOT, P, D)
    return out.reshape(2, S_LEN, D)
```
